# Optimizing a Trainium2 kernel written in Bass

```python
import math
import jax
import jax.numpy as jnp
from jax import lax
import numpy as np


D_MODEL = 2048
BATCH = 4
SEQ = 8192
DEPTH = 1

M_HEADS = 4
M_DQK = 128
M_DV = 256
M_CHUNK = 128
GATE_CAP = 15.0
A_HEADS = 8
A_DH = 64
A_DV = 2 * A_DH
Q_BLOCK = 128
ROPE_THETA = 10000.0
D_FF = -(-8 * D_MODEL // (3 * 256)) * 256
N_BRANCH = 2
EPS = 1e-6
M_WIDTH = M_HEADS * M_DV
A_WIDTH = A_HEADS * A_DV
SPLIT_SIZES = (M_HEADS * M_DQK, M_HEADS * M_DQK, M_WIDTH, M_WIDTH, 4 * M_HEADS,
               A_HEADS * 2 * A_DH, A_HEADS * 2 * A_DH, A_WIDTH, N_BRANCH * D_MODEL)
D_IN = sum(SPLIT_SIZES)
IN_OFFSETS = tuple(int(o) for o in np.cumsum(SPLIT_SIZES)[:-1])

kernel_name = 'hybrid_mlstm_diffattn_encoder'


def _rmsnorm(x, w):
    xf = x.astype(jnp.float32)
    y = xf * lax.rsqrt(jnp.mean(xf * xf, axis=-1, keepdims=True) + EPS)
    return (y * w.astype(jnp.float32)).astype(x.dtype)


def _softcap(t):
    return GATE_CAP * jnp.tanh(t / GATE_CAP)


def _rope(t):
    S = t.shape[1]
    d = t.shape[-1]
    inv = ROPE_THETA ** (-jnp.arange(0, d, 2, dtype=jnp.float32) / d)
    ang = jnp.arange(S, dtype=jnp.float32)[:, None] * inv[None, :]
    cos = jnp.cos(ang)[:, None, None, :]
    sin = jnp.sin(ang)[:, None, None, :]
    tf = t.astype(jnp.float32)
    t1, t2 = tf[..., : d // 2], tf[..., d // 2:]
    return jnp.concatenate([t1 * cos - t2 * sin, t1 * sin + t2 * cos], axis=-1).astype(t.dtype)


def _mlstm_scan(q, k, v, ig, lf):
    B, H, S, Dk = q.shape
    Dv = v.shape[-1]
    L = M_CHUNK
    nc = S // L

    def to_chunks(t):
        return jnp.moveaxis(t.astype(jnp.float32).reshape((B, H, nc, L) + t.shape[3:]), 2, 0)

    xs = tuple(to_chunks(t) for t in (q, k, v, ig, lf))
    tril = jnp.tril(jnp.ones((L, L), dtype=bool))

    def step(carry, inp):
        C, n, m = carry
        qc, kc, vc, igc, lfc = inp
        b = jnp.cumsum(lfc, axis=-1)
        dmat = b[..., :, None] - b[..., None, :] + igc[..., None, :]
        dmat = jnp.where(tril, dmat, -jnp.inf)
        inter = b + m[..., None]
        m_t = jnp.maximum(inter, jnp.max(dmat, axis=-1))
        wts = jnp.exp(dmat - m_t[..., None])
        a = jnp.exp(inter - m_t)
        s = jnp.einsum('bhtd,bhsd->bhts', qc, kc) * wts
        num = a[..., None] * jnp.einsum('bhvd,bhtd->bhtv', C, qc) + jnp.einsum('bhts,bhsv->bhtv', s, vc)
        den = a * jnp.einsum('bhd,bhtd->bht', n, qc) + jnp.sum(s, axis=-1)
        h = num / jnp.maximum(jnp.abs(den), jnp.exp(-m_t))[..., None]
        bL = b[..., -1]
        g = bL[..., None] - b + igc
        m_new = jnp.maximum(bL + m, jnp.max(g, axis=-1))
        decay = jnp.exp(bL + m - m_new)
        wk = jnp.exp(g - m_new[..., None])
        C = decay[..., None, None] * C + jnp.einsum('bhsv,bhsd->bhvd', vc, kc * wk[..., None])
        n = decay[..., None] * n + jnp.einsum('bhs,bhsd->bhd', wk, kc)
        return (C, n, m_new), h

    init = (jnp.zeros((B, H, Dv, Dk), jnp.float32),
            jnp.zeros((B, H, Dk), jnp.float32),
            jnp.zeros((B, H), jnp.float32))
    _, hs = lax.scan(step, init, xs)
    return jnp.moveaxis(hs, 0, 2).reshape(B, H, S, Dv)


def _diff_attention(q, k, v, lam):
    B, S, H, _, dh = q.shape
    nb = S // Q_BLOCK
    qb = jnp.moveaxis(q.reshape(B, nb, Q_BLOCK, H, 2, dh), 1, 0)
    scale = dh ** -0.5

    def block(qi):
        s = jnp.einsum('bqhcd,bkhcd->bhcqk', qi, k).astype(jnp.float32) * scale
        p = jax.nn.softmax(s, axis=-1)
        p = p[:, :, 0] - lam * p[:, :, 1]
        return jnp.einsum('bhqk,bkhv->bqhv', p.astype(v.dtype), v)

    o = lax.map(block, qb)
    return jnp.moveaxis(o, 0, 1).reshape(B, S, H, v.shape[-1])


def setup_inputs(seed: int = 0) -> dict:
    key = jax.random.key(seed)
    ks = jax.random.split(key, 20)
    f32 = jnp.float32

    def w(k, shape, fan_in):
        return jax.random.normal(k, shape, f32) * fan_in ** -0.5

    def gain(k, shape):
        return 1.0 + 0.02 * jax.random.normal(k, shape, f32)

    x = jax.random.normal(ks[0], (BATCH, SEQ, D_MODEL), f32)
    norm1_w = gain(ks[1], (DEPTH, D_MODEL))
    w_in = w(ks[2], (DEPTH, D_MODEL, D_IN), D_MODEL)
    b_igate = 0.1 * jax.random.normal(ks[3], (DEPTH, 2, M_HEADS), f32)
    b_fgate = jnp.linspace(3.0, 6.0, M_HEADS, dtype=f32)[None, None, :] + 0.1 * jax.random.normal(ks[4], (DEPTH, 2, M_HEADS), f32)
    b_branch_gate = 0.1 * jax.random.normal(ks[5], (DEPTH, N_BRANCH * D_MODEL), f32)
    mlstm_norm_w = gain(ks[6], (DEPTH, M_WIDTH))
    lam_q1 = 0.1 * jax.random.normal(ks[7], (DEPTH, A_DH), f32)
    lam_k1 = 0.1 * jax.random.normal(ks[8], (DEPTH, A_DH), f32)
    lam_q2 = 0.1 * jax.random.normal(ks[9], (DEPTH, A_DH), f32)
    lam_k2 = 0.1 * jax.random.normal(ks[10], (DEPTH, A_DH), f32)
    attn_norm_w = gain(ks[11], (DEPTH, A_DV))
    w_branch_m = w(ks[12], (DEPTH, M_WIDTH, D_MODEL), M_WIDTH)
    w_branch_a = w(ks[13], (DEPTH, A_WIDTH, D_MODEL), A_WIDTH)
    w_out = w(ks[14], (DEPTH, D_MODEL, D_MODEL), D_MODEL)
    norm2_w = gain(ks[15], (DEPTH, D_MODEL))
    w_ffn_in = w(ks[16], (DEPTH, D_MODEL, 2 * D_FF), D_MODEL)
    w_ffn_out = w(ks[17], (DEPTH, D_FF, D_MODEL), D_FF)
    final_norm_w = gain(ks[18], (D_MODEL,))
    return {'x': x, 'norm1_w': norm1_w, 'w_in': w_in, 'b_igate': b_igate, 'b_fgate': b_fgate,
            'b_branch_gate': b_branch_gate, 'mlstm_norm_w': mlstm_norm_w,
            'lam_q1': lam_q1, 'lam_k1': lam_k1, 'lam_q2': lam_q2, 'lam_k2': lam_k2,
            'attn_norm_w': attn_norm_w, 'w_branch_m': w_branch_m, 'w_branch_a': w_branch_a,
            'w_out': w_out, 'norm2_w': norm2_w, 'w_ffn_in': w_ffn_in, 'w_ffn_out': w_ffn_out,
            'final_norm_w': final_norm_w}


def reference(x, norm1_w, w_in, b_igate, b_fgate, b_branch_gate, mlstm_norm_w,
              lam_q1, lam_k1, lam_q2, lam_k2, attn_norm_w, w_branch_m, w_branch_a,
              w_out, norm2_w, w_ffn_in, w_ffn_out, final_norm_w):
    B, S, D = x.shape
    for l in range(DEPTH):
        h = _rmsnorm(x, norm1_w[l])
        proj = h @ w_in[l]
        mq, mk, mv, mo, mg, aq, ak, av, gt = jnp.split(proj, IN_OFFSETS, axis=-1)

        q = mq.reshape(B, S, M_HEADS, M_DQK).transpose(0, 2, 1, 3)
        k = mk.reshape(B, S, M_HEADS, M_DQK).transpose(0, 2, 1, 3) * (M_DQK ** -0.5)
        v = mv.reshape(B, S, M_HEADS, M_DV).transpose(0, 2, 1, 3)
        g = mg.astype(jnp.float32).reshape(B, S, 2, 2, M_HEADS)
        ig = _softcap(g[:, :, :, 0] + b_igate[l].astype(jnp.float32))
        lf = jax.nn.log_sigmoid(_softcap(g[:, :, :, 1] + b_fgate[l].astype(jnp.float32)))
        ig = ig.transpose(2, 0, 3, 1)
        lf = lf.transpose(2, 0, 3, 1)
        h_fwd = _mlstm_scan(q, k, v, ig[0], lf[0])
        h_bwd = jnp.flip(_mlstm_scan(jnp.flip(q, 2), jnp.flip(k, 2), jnp.flip(v, 2),
                                     jnp.flip(ig[1], 2), jnp.flip(lf[1], 2)), 2)
        hm = (h_fwd + h_bwd).astype(x.dtype).transpose(0, 2, 1, 3)
        hm = _rmsnorm(hm, mlstm_norm_w[l].reshape(M_HEADS, M_DV))
        hm = hm * jax.nn.sigmoid(mo).reshape(B, S, M_HEADS, M_DV)
        branch_m = hm.reshape(B, S, M_WIDTH) @ w_branch_m[l]

        lam_init = 0.8 - 0.6 * math.exp(-0.3 * l)
        lam = (jnp.exp(jnp.dot(lam_q1[l].astype(jnp.float32), lam_k1[l].astype(jnp.float32)))
               - jnp.exp(jnp.dot(lam_q2[l].astype(jnp.float32), lam_k2[l].astype(jnp.float32)))
               + lam_init)
        qa = _rope(aq.reshape(B, S, A_HEADS, 2, A_DH))
        ka = _rope(ak.reshape(B, S, A_HEADS, 2, A_DH))
        va = av.reshape(B, S, A_HEADS, A_DV)
        ha = _diff_attention(qa, ka, va, lam)
        ha = _rmsnorm(ha, attn_norm_w[l]) * (1.0 - lam_init)
        branch_a = ha.reshape(B, S, A_WIDTH) @ w_branch_a[l]

        gates = jax.nn.sigmoid(gt + b_branch_gate[l])
        g_m, g_a = jnp.split(gates, N_BRANCH, axis=-1)
        x = x + (g_m * branch_m + g_a * branch_a) @ w_out[l]

        h2 = _rmsnorm(x, norm2_w[l])
        gate, up = jnp.split(h2 @ w_ffn_in[l], 2, axis=-1)
        x = x + (jax.nn.silu(gate) * up) @ w_ffn_out[l]
    return _rmsnorm(x, final_norm_w)
```

```python
import math
import bisect
from contextlib import ExitStack
import numpy as np
import concourse.bass as bass
import concourse.mybir as mybir
from concourse.bass_utils import run_bass_kernel_spmd

F32 = mybir.dt.float32
BF16 = mybir.dt.bfloat16
AF = mybir.ActivationFunctionType
ALU = mybir.AluOpType
AX = mybir.AxisListType

D = 2048
KC = D // 128
MH, MDK, MDV = 4, 128, 256
AH, ADH, ADV = 8, 64, 128
DFF = 5632
FC = DFF // 128
EPS = 1e-6
CAP = 15.0
LAM_INIT = 0.8 - 0.6 * math.exp(0.0)
ROPE_THETA = 10000.0
SEQ = 8192
O_MQ, O_MK, O_MV, O_MO, O_MG, O_AQ, O_AK, O_AV, O_GT = 0, 512, 1024, 2048, 3072, 3088, 4112, 5136, 6160
N_FM = 24
SPLIT_PER = {"x": 1024, "win_fm": 6, "win_tm": 3, "win_gt": 8, "wbm": 16, "wba": 16, "wout": 2, "wfi": 8, "wfo": 1}
TMW = 3088


class Prog:
    def __init__(self, nc):
        self.nc = nc
        self.ops = []

    def op(self, eng, fn, r=(), w=(), dma=None):
        w = tuple(w) + tuple(k for k in r if k.startswith("ps") and k not in w)
        self.ops.append(dict(eng=eng, fn=fn, r=tuple(r), w=w, dma=dma))

    def barrier(self):
        for e in ("pe", "act", "dve", "pool", "sp"):
            self.ops.append(dict(eng=e, fn=None, r=(), w=(), dma=None, bar=True))

    def emit(self):
        nc = self.nc
        ops = self.ops
        writers, readers = {}, {}
        deps = []
        last_eng, last_dma = {}, {}
        for i, o in enumerate(ops):
            d = {}
            if o.get("bar"):
                for j in list(last_eng.values()) + list(last_dma.values()):
                    d[j] = "raw"
            elif o["dma"] is not None:
                last_dma[o["dma"]] = i
            elif o["fn"] is not None:
                last_eng[o["eng"]] = i
            for k in o["r"]:
                for j in writers.get(k, ()):
                    d[j] = "raw"
            for k in o["w"]:
                rl = readers.get(k, [])
                wl = writers.get(k, [])
                for j in wl:
                    d.setdefault(j, "waw")
                for j in rl:
                    d.setdefault(j, "war")
            for k in o["r"]:
                readers.setdefault(k, []).append(i)
            for k in o["w"]:
                if readers.get(k):
                    writers[k] = [i]
                    readers[k] = []
                else:
                    writers.setdefault(k, []).append(i)
            dd = []
            for j, kind in d.items():
                if j == i:
                    continue
                p = ops[j]
                if o.get("bar"):
                    if p["dma"] is None and p["eng"] == o["eng"]:
                        continue
                    dd.append(j)
                    continue
                if p["dma"] is None and o["dma"] is None and p["eng"] == o["eng"] and o["eng"] == "pe":
                    continue
                dd.append(j)
            deps.append(dd)
        signaled = [False] * len(ops)
        for dd in deps:
            for j in dd:
                signaled[j] = True
        engs = ["pe", "act", "dve", "pool", "sp"]
        sems = {e: nc.alloc_semaphore("c_" + e) for e in engs}
        cnt = {e: 0 for e in engs}
        dcnt = {}
        ev = [None] * len(ops)
        for i, o in enumerate(ops):
            if o["dma"] is not None:
                k = o["dma"]
                if k not in sems:
                    sems[k] = nc.alloc_semaphore("d_" + k)
                    dcnt[k] = 0
                dcnt[k] += 16
                ev[i] = (k, dcnt[k])
            elif signaled[i] and o["fn"] is not None:
                cnt[o["eng"]] += 1
                ev[i] = (o["eng"], cnt[o["eng"]])
        per = {e: [] for e in engs}
        dma_hist = {}
        for i, o in enumerate(ops):
            per[o["eng"]].append(i)
            if o["dma"] is not None:
                h = dma_hist.setdefault(o["dma"], ([], []))
                h[0].append(i)
                h[1].append(ev[i][1])
        self.stats = {e: len(per[e]) for e in engs}
        self.nsem = len(sems)

        def run(e, name):
            waited = {}
            for i in per[name]:
                o = ops[i]
                need = {}
                for j in deps[i]:
                    s, v = ev[j]
                    if ops[j]["dma"] is not None:
                        idxs, vals = dma_hist[s]
                        v = vals[bisect.bisect_left(idxs, i) - 1]
                    if v > need.get(s, 0):
                        need[s] = v
                for s, v in need.items():
                    if v > waited.get(s, 0):
                        e.wait_ge(sems[s], v)
                        waited[s] = v
                if o["fn"] is None:
                    continue
                ins = o["fn"](e)
                if o["dma"] is not None:
                    ins.then_inc(sems[o["dma"]], 16)
                elif signaled[i]:
                    ins.then_inc(sems[name], 1)

        with nc.Block() as block:
            @block.tensor
            def _(e):
                run(e, "pe")

            @block.scalar
            def _(e):
                run(e, "act")

            @block.vector
            def _(e):
                run(e, "dve")

            @block.gpsimd
            def _(e):
                run(e, "pool")

            @block.sync
            def _(e):
                run(e, "sp")


def build(T2, T=None, dbg=(), stop=None):
    if T is None:
        T = T2 // 2
    full = (T == T2)
    NB2, NB = T2 // 512, T // 512
    NC2, NCH = T2 // 128, T // 128
    nc = bass.Bass("TRN2", target_bir_lowering=False)
    P = Prog(nc)
    used_inputs = []

    def din(name, shape, dt=F32):
        used_inputs.append(name)
        return nc.dram_tensor(name, list(shape), dt, kind="ExternalInput").ap()

    class SplitAP:
        def __init__(self, name, shape, per):
            self.per = per
            n = shape[0] // per
            self.pieces = [din("%s_%d" % (name, k), [per] + list(shape[1:])) for k in range(n)]

        def __getitem__(self, idx):
            if isinstance(idx, tuple):
                return self.pieces[idx[0] // self.per][(idx[0] % self.per,) + tuple(idx[1:])]
            return self.pieces[idx // self.per][idx % self.per]

        def rows(self, r0, r1):
            k = r0 // self.per
            assert (r1 - 1) // self.per == k
            return self.pieces[k][r0 - k * self.per:r1 - k * self.per, :]

    def dscr(name, shape, dt):
        kind = "ExternalOutput" if name in dbg else "Internal"
        return nc.dram_tensor(name, list(shape), dt, kind=kind).ap()

    def gsb(name, shape, dt=F32):
        return nc.alloc_sbuf_tensor(name, list(shape), dt)

    final_keys = []
    wkeys = {}

    def skey(name, blk):
        k = "%s:%d" % (name, blk)
        if k not in final_keys:
            final_keys.append(k)
        return k

    def finish():
        P.op("sp", None, r=list(final_keys))
        P.emit()
        P.used_inputs = used_inputs
        return nc, P

    x_d = SplitAP("x", [T2, D], SPLIT_PER["x"])
    win_fm_d = SplitAP("win_fm", [N_FM, 128, KC, 128], SPLIT_PER["win_fm"])
    win_tm_d = SplitAP("win_tm", [6, 128, KC, 512], SPLIT_PER["win_tm"])
    win_tg_d = din("win_tg", [128, KC, 16])
    n1w_d = din("n1w", [128, KC])
    gbias_d = din("gbias", [1, 16])
    ropec_d = din("ropec", [128, T2])
    ropes_d = din("ropes", [128, T2])
    cst_d = din("cst", [128, 4 * 128])
    out_d = nc.dram_tensor("out", [T, D], F32, kind="ExternalOutput").ap()

    wb_fm = dscr("wb_fm", [N_FM, 128, KC, 128], BF16)
    wb_tm = dscr("wb_tm", [6, 128, KC, 512], BF16)
    wb_tg = dscr("wb_tg", [128, KC, 16], BF16)
    hT_s = dscr("hT_s", [128, KC, T], BF16)
    mqT_s = dscr("mqT_s", [128, MH, T], BF16)
    mkT_s = dscr("mkT_s", [128, MH, T2], BF16)
    mV_s = dscr("mV_s", [T2, MH * MDV], BF16)
    mo_s = dscr("mo_s", [T, MH * MDV], F32)
    g_s = dscr("g_s", [T2, 16], F32)
    aqT_s = dscr("aqT_s", [128, AH, T], BF16)
    akT_s = dscr("akT_s", [128, AH, T2], BF16)
    aV_s = dscr("aV_s", [T2, AH * ADV], BF16)
    hA_s = dscr("hA_s", [T, MH * MDV], F32)
    hmT_s = dscr("hmT_s", [128, 8, T], BF16)
    haT_s = dscr("haT_s", [128, 8, T], BF16)

    cst_f = gsb("cst_f", [128, 512])
    ident = gsb("ident", [128, 128], BF16)
    perm = gsb("perm", [128, 128], BF16)
    ones_b = gsb("ones_b", [128, 128], BF16)
    ones_f = gsb("ones_f", [128, 128])
    n1w = gsb("n1w_sb", [128, KC])
    gbias = gsb("gbias_sb", [128, 16])
    identf = cst_f[:, 0:128]
    Uf = cst_f[:, 256:384]
    Lf = cst_f[:, 384:512]
    P.op("sp", lambda e: e.dma_start(out=cst_f[:], in_=cst_d), w=["cst_f", "once"], dma="once")
    P.op("sp", lambda e: e.dma_start(out=n1w[:], in_=n1w_d), w=["n1w", "once"], dma="once")
    P.op("sp", lambda e: e.dma_start(out=gbias[:], in_=gbias_d.partition_broadcast(128)), w=["gbias", "once"], dma="once")
    P.op("dve", lambda e: e.tensor_copy(out=ident[:], in_=cst_f[:, 0:128]), r=["cst_f"], w=["ident"])
    P.op("dve", lambda e: e.tensor_copy(out=perm[:], in_=cst_f[:, 128:256]), r=["cst_f"], w=["perm"])
    P.op("dve", lambda e: e.memset(ones_b[:], 1.0), w=["ones_b"])
    P.op("dve", lambda e: e.memset(ones_f[:], 1.0), w=["ones_f"])

    pp = [nc.alloc_psum_tensor("pp%d" % i, [128, 1024], F32) for i in range(4)]
    ps = []
    for i in range(4):
        ps.append(pp[i][:, 0:512])
        ps.append(pp[i][:, 512:1024])

    def psk(*idx):
        return ["ps%d" % i for i in idx]

    cvst = ExitStack()
    cv_in = [cvst.enter_context(nc.sbuf_tensor("cv_in%d" % i, [128, 2048], F32)) for i in range(2)]
    cv_out = [cvst.enter_context(nc.sbuf_tensor("cv_out%d" % i, [128, 2048], BF16)) for i in range(2)]
    cvn = [0]

    def convert_steps(src, dst, key0):
        Fdim = src.shape[1]
        wkeys[key0] = []
        chunks = []
        for c0 in range(0, Fdim, 2048):
            key = "%s:%d" % (key0, c0 // 2048)
            wkeys[key0].append(key)
            final_keys.append(key)
            chunks.append((c0, min(2048, Fdim - c0), key))

        def gen():
            for (c0, cw, key) in chunks:
                s = cvn[0] % 2
                cvn[0] += 1
                P.op("sp", lambda e, s=s, c0=c0, cw=cw: e.dma_start(out=cv_in[s][:, 0:cw], in_=src[:, c0:c0 + cw]),
                     w=["cv_in%d" % s], dma="cv_in%d" % s)
                P.op("pool", lambda e, s=s, cw=cw: e.tensor_copy(out=cv_out[s][:, 0:cw], in_=cv_in[s][:, 0:cw]),
                     r=["cv_in%d" % s], w=["cv_out%d" % s])
                P.op("sp", lambda e, s=s, c0=c0, cw=cw: e.dma_start(out=dst[:, c0:c0 + cw], in_=cv_out[s][:, 0:cw]),
                     r=["cv_out%d" % s], w=[key], dma="cv_out%d" % s)
                yield
        return gen()

    def flat(ap):
        names = "abcdefg"[:len(ap.shape) - 1]
        return ap.rearrange("p %s -> p (%s)" % (" ".join(names), " ".join(names)))

    for j in range(N_FM):
        for _ in convert_steps(flat(win_fm_d[j]), flat(wb_fm[j]), "wb_fm%d" % j):
            pass
    for n in range(6):
        for _ in convert_steps(flat(win_tm_d[n]), flat(wb_tm[n]), "wb_tm%d" % n):
            pass
    for _ in convert_steps(flat(win_tg_d), flat(wb_tg), "wb_tg"):
        pass
    if stop == "p0":
        return finish()

    lazy = []
    if stop is None or stop in ("p3", "p4", "p5"):
        win_gt_d = SplitAP("win_gt", [32, 128, KC, 128], SPLIT_PER["win_gt"])
        wbm_d = SplitAP("wbm", [16, 128, 8, 128], SPLIT_PER["wbm"])
        wba_d = SplitAP("wba", [16, 128, 8, 128], SPLIT_PER["wba"])
        wout_d = SplitAP("wout", [4, 128, KC, 512], SPLIT_PER["wout"])
        wfi_d = SplitAP("wfi", [2 * FC, 128, KC, 128], SPLIT_PER["wfi"])
        wfo_d = SplitAP("wfo", [4, 4, 128, 11, 512], SPLIT_PER["wfo"])
        wb_gt = dscr("wb_gt", [32, 128, KC, 128], BF16)
        wb_bm = dscr("wb_bm", [16, 128, 8, 128], BF16)
        wb_ba = dscr("wb_ba", [16, 128, 8, 128], BF16)
        wb_out = dscr("wb_out", [4, 128, KC, 512], BF16)
        wb_fi = dscr("wb_fi", [2 * FC, 128, KC, 128], BF16)
        wb_fo = dscr("wb_fo", [4, 4, 128, 11, 512], BF16)
        for j in range(16):
            lazy.append(convert_steps(flat(wbm_d[j]), flat(wb_bm[j]), "wb_bm%d" % j))
            lazy.append(convert_steps(flat(wba_d[j]), flat(wb_ba[j]), "wb_ba%d" % j))
        for j in range(32):
            lazy.append(convert_steps(flat(win_gt_d[j]), flat(wb_gt[j]), "wb_gt%d" % j))
        for n in range(4):
            lazy.append(convert_steps(flat(wout_d[n]), flat(wb_out[n]), "wb_out%d" % n))
        for j in range(2 * FC):
            lazy.append(convert_steps(flat(wfi_d[j]), flat(wb_fi[j]), "wb_fi%d" % j))
        for n in range(4):
            for g in range(4):
                lazy.append(convert_steps(flat(wfo_d[n, g]), flat(wb_fo[n, g]), "wb_fo%d_%d" % (n, g)))

    def lazy_step(n=1):
        for _ in range(n):
            while lazy:
                try:
                    next(lazy[0])
                    break
                except StopIteration:
                    lazy.pop(0)

    with ExitStack() as st:
        def sb(name, shape, dt=F32):
            return st.enter_context(nc.sbuf_tensor(name, list(shape), dt))
        xt = [sb("xt%d" % i, [128, D]) for i in range(2)]
        xn = [sb("xn%d" % i, [128, D], BF16) for i in range(2)]
        ss = [sb("ss%d" % i, [128, 1]) for i in range(2)]
        rs = [sb("rs%d" % i, [128, 1]) for i in range(2)]
        hTb = [sb("hTb%d" % i, [128, KC, 512], BF16) for i in range(2)]
        wfm = [sb("wfm%d" % i, [128, KC, 128], BF16) for i in range(3)]
        wtm = [sb("wtm%d" % i, [128, KC, 512], BF16) for i in range(2)]
        wtg = sb("wtg", [128, KC, 16], BF16)
        rc = [sb("rc%d" % i, [128, 512]) for i in range(2)]
        rsn = [sb("rsn%d" % i, [128, 512]) for i in range(2)]
        fmo = [sb("fmo%d" % i, [128, 512], BF16) for i in range(3)]
        qraw = [sb("qraw%d" % i, [128, 512], BF16) for i in range(2)]
        qc = [sb("qc%d" % i, [128, 512]) for i in range(2)]
        qs = [sb("qs%d" % i, [128, 512]) for i in range(2)]
        tmo_b = [sb("tmo_b%d" % i, [128, 512], BF16) for i in range(2)]
        tmo_f = [sb("tmo_f%d" % i, [128, 512]) for i in range(2)]
        gto = [sb("gto%d" % i, [128, 16]) for i in range(2)]
        P.op("sp", lambda e: e.dma_start(out=wtg[:], in_=wb_tg), r=wkeys["wb_tg"], w=["wtg", "once"], dma="once")
        n_wfm, n_wtm, n_fmo, n_ps, n_rope, n_tmo = [0], [0], [0], [0], [0], [0]

        def mm_bank():
            b = 2 + (n_ps[0] % 6)
            n_ps[0] += 1
            return b

        n_lazy_chunks = 16 * 2 + 32 * 2 + 4 * 8 + 2 * FC * 2 + 16 * 3
        hooks = NB2 * 30
        per_hook = -(-n_lazy_chunks // hooks) if lazy else 0

        for blk in range(NB2):
            own = blk < NB
            hs = blk % 2
            t0 = blk * 512
            for sub in range(4):
                tt = blk * 4 + sub
                s = tt % 2
                P.op("sp", lambda e, s=s, tt=tt: e.dma_start(out=xt[s][:], in_=x_d.rows(tt * 128, (tt + 1) * 128)),
                     w=["xt%d" % s], dma="xt%d" % s)
                P.op("act", lambda e, s=s: e.activation(out=xn[s][:], in_=xt[s][:], func=AF.Square, accum_out=ss[s][:]),
                     r=["xt%d" % s], w=["xn%d" % s, "ss%d" % s])
                P.op("dve", lambda e, s=s: e.tensor_scalar(out=rs[s][:], in0=ss[s][:], scalar1=1.0 / D, scalar2=EPS,
                                                          op0=ALU.mult, op1=ALU.add),
                     r=["ss%d" % s], w=["rs%d" % s])
                P.op("act", lambda e, s=s: e.sqrt(out=rs[s][:], in_=rs[s][:]), r=["rs%d" % s], w=["rs%d" % s])
                P.op("dve", lambda e, s=s: e.reciprocal(out=rs[s][:], in_=rs[s][:]), r=["rs%d" % s], w=["rs%d" % s])
                P.op("dve", lambda e, s=s: e.tensor_scalar(out=xn[s][:], in0=xt[s][:], scalar1=rs[s][:], scalar2=None,
                                                          op0=ALU.mult),
                     r=["xt%d" % s, "rs%d" % s], w=["xn%d" % s])

                def tr(e, s=s, half=0):
                    ins = None
                    pbv = ps[half].bitcast(BF16)
                    for c in range(8):
                        cc = half * 8 + c
                        ins = e.transpose(out=pbv[:, c * 128:(c + 1) * 128], in_=xn[s][:, cc * 128:(cc + 1) * 128],
                                          identity=ident[:])
                    return ins
                for half in range(2):
                    P.op("pe", lambda e, f=tr, half=half: f(e, half=half), r=["xn%d" % s, "ident"], w=["ps%d" % half])
                    P.op("dve", lambda e, half=half, hs=hs, sub=sub: e.tensor_tensor(
                        out=hTb[hs][:, half * 8:(half + 1) * 8, sub * 128:(sub + 1) * 128],
                        in0=ps[half].bitcast(BF16).rearrange("p (c t) -> p c t", c=8),
                        in1=n1w[:, half * 8:(half + 1) * 8].unsqueeze(2).to_broadcast([128, 8, 128]),
                        op=ALU.mult),
                        r=["ps%d" % half, "n1w"], w=["hTb%d" % hs])
            if own:
                P.op("sp", lambda e, hs=hs, t0=t0: e.dma_start(out=hT_s[:, :, t0:t0 + 512], in_=hTb[hs][:]),
                     r=["hTb%d" % hs], w=[skey("hT_s", blk)], dma="hTb_st%d" % hs)
            if stop == "norm":
                continue
            rp = blk % 2
            P.op("sp", lambda e, rp=rp, t0=t0: e.dma_start(out=rc[rp][:], in_=ropec_d[:, t0:t0 + 512]),
                 w=["rc%d" % rp, "rsn%d" % rp], dma="rope%d" % rp)
            P.op("sp", lambda e, rp=rp, t0=t0: e.dma_start(out=rsn[rp][:], in_=ropes_d[:, t0:t0 + 512]),
                 w=["rc%d" % rp, "rsn%d" % rp], dma="rope%d" % rp)
            fm_list = list(range(N_FM)) if own else [4, 5, 6, 7] + list(range(16, 24))
            for j in fm_list:
                lazy_step(per_hook)
                ws = n_wfm[0] % 3
                n_wfm[0] += 1
                P.op("sp", lambda e, ws=ws, j=j: e.dma_start(out=wfm[ws][:], in_=wb_fm[j]),
                     r=wkeys["wb_fm%d" % j], w=["wfm%d" % ws], dma="wfm%d" % ws)
                b = mm_bank()

                def mm(e, ws=ws, b=b, hs=hs):
                    ins = None
                    for kc in range(KC):
                        ins = e.matmul(ps[b], lhsT=wfm[ws][:, kc, :], rhs=hTb[hs][:, kc, :],
                                       start=(kc == 0), stop=(kc == KC - 1))
                    return ins
                P.op("pe", mm, r=["wfm%d" % ws, "hTb%d" % hs], w=["ps%d" % b])
                fo = n_fmo[0] % 3
                n_fmo[0] += 1
                if j < 8:
                    if j < 4:
                        dst, sc, dk = mqT_s[:, j, t0:t0 + 512], 1.0, skey("mqT_s", blk)
                    else:
                        dst, sc, dk = mkT_s[:, j - 4, t0:t0 + 512], MDK ** -0.5, skey("mkT_s", blk)
                    P.op("act", lambda e, fo=fo, b=b, sc=sc: e.mul(out=fmo[fo][:], in_=ps[b], mul=sc),
                         r=["ps%d" % b], w=["fmo%d" % fo])
                else:
                    ri = n_rope[0] % 2
                    n_rope[0] += 1
                    if j < 16:
                        dst, dk = aqT_s[:, j - 8, t0:t0 + 512], skey("aqT_s", blk)
                    else:
                        dst, dk = akT_s[:, j - 16, t0:t0 + 512], skey("akT_s", blk)
                    P.op("act", lambda e, ri=ri, b=b: e.copy(out=qraw[ri][:], in_=ps[b]),
                         r=["ps%d" % b], w=["qraw%d" % ri])
                    P.op("dve", lambda e, ri=ri, b=b, rp=rp: e.tensor_tensor(out=qc[ri][:], in0=ps[b], in1=rc[rp][:],
                                                                            op=ALU.mult),
                         r=["ps%d" % b, "rc%d" % rp], w=["qc%d" % ri])
                    b2 = mm_bank()
                    P.op("pe", lambda e, ri=ri, b2=b2: e.matmul(ps[b2], lhsT=perm[:], rhs=qraw[ri][:], start=True, stop=True),
                         r=["perm", "qraw%d" % ri], w=["ps%d" % b2])
                    P.op("dve", lambda e, ri=ri, b2=b2, rp=rp: e.tensor_tensor(out=qs[ri][:], in0=ps[b2],
                                                                              in1=rsn[rp][:], op=ALU.mult),
                         r=["ps%d" % b2, "rsn%d" % rp], w=["qs%d" % ri])
                    P.op("dve", lambda e, ri=ri, fo=fo: e.tensor_tensor(out=fmo[fo][:], in0=qs[ri][:], in1=qc[ri][:],
                                                                        op=ALU.add),
                         r=["qs%d" % ri, "qc%d" % ri], w=["fmo%d" % fo])
                P.op("sp", lambda e, fo=fo, dst=dst: e.dma_start(out=dst, in_=fmo[fo][:]),
                     r=["fmo%d" % fo], w=[dk], dma="fmo%d" % fo)
            if stop == "fm":
                continue
            tm_list = [0, 1, 2, 3, 4, 5] if own else [0, 1, 2, 3]
            for n in tm_list:
                lazy_step(per_hook)
                ws = n_wtm[0] % 2
                n_wtm[0] += 1
                P.op("sp", lambda e, ws=ws, n=n: e.dma_start(out=wtm[ws][:], in_=wb_tm[n]),
                     r=wkeys["wb_tm%d" % n], w=["wtm%d" % ws], dma="wtm%d" % ws)
                for sub in range(4):
                    b = mm_bank()

                    def mm2(e, ws=ws, b=b, hs=hs, sub=sub):
                        ins = None
                        for kc in range(KC):
                            ins = e.matmul(ps[b], lhsT=hTb[hs][:, kc, sub * 128:(sub + 1) * 128],
                                           rhs=wtm[ws][:, kc, :], start=(kc == 0), stop=(kc == KC - 1))
                        return ins
                    P.op("pe", mm2, r=["wtm%d" % ws, "hTb%d" % hs], w=["ps%d" % b])
                    to = n_tmo[0] % 2
                    n_tmo[0] += 1
                    r0 = t0 + sub * 128
                    c0 = (n % 2) * 512
                    if n < 4:
                        dstT = mV_s if n < 2 else aV_s
                        dk = skey("mV_s" if n < 2 else "aV_s", blk)
                        P.op("act", lambda e, to=to, b=b: e.copy(out=tmo_b[to][:], in_=ps[b]),
                             r=["ps%d" % b], w=["tmo_b%d" % to])
                        P.op("sp", lambda e, to=to, dstT=dstT, r0=r0, c0=c0: e.dma_start(
                            out=dstT[r0:r0 + 128, c0:c0 + 512], in_=tmo_b[to][:]),
                            r=["tmo_b%d" % to], w=[dk], dma="tmo_b%d" % to)
                    else:
                        P.op("act", lambda e, to=to, b=b: e.activation(out=tmo_f[to][:], in_=ps[b], func=AF.Sigmoid),
                             r=["ps%d" % b], w=["tmo_f%d" % to])
                        P.op("sp", lambda e, to=to, r0=r0, c0=c0: e.dma_start(
                            out=mo_s[r0:r0 + 128, c0:c0 + 512], in_=tmo_f[to][:]),
                            r=["tmo_f%d" % to], w=[skey("mo_s", blk)], dma="tmo_f%d" % to)
            for sub in range(4):
                b = mm_bank()
                gi = sub % 2

                def mm3(e, b=b, hs=hs, sub=sub):
                    ins = None
                    for kc in range(KC):
                        ins = e.matmul(ps[b][:, 0:16], lhsT=hTb[hs][:, kc, sub * 128:(sub + 1) * 128], rhs=wtg[:, kc, :],
                                       start=(kc == 0), stop=(kc == KC - 1))
                    return ins
                P.op("pe", mm3, r=["wtg", "hTb%d" % hs], w=["ps%d" % b])
                r0 = t0 + sub * 128
                P.op("dve", lambda e, gi=gi, b=b: e.tensor_tensor(out=gto[gi][:], in0=ps[b][:, 0:16], in1=gbias[:],
                                                                  op=ALU.add),
                     r=["ps%d" % b, "gbias"], w=["gto%d" % gi])
                P.op("sp", lambda e, gi=gi, r0=r0: e.dma_start(out=g_s[r0:r0 + 128, :], in_=gto[gi][:]),
                     r=["gto%d" % gi], w=[skey("g_s", blk)], dma="gto%d" % gi)
        lazy_step(100000)
    P.barrier()
    cvst.close()
    if stop in ("norm", "fm", "p2"):
        return finish()

    mnw_d = din("mnw", [1, MH * MDV])
    with ExitStack() as st:
        def sb(name, shape, dt=F32):
            return st.enter_context(nc.sbuf_tensor(name, list(shape), dt))
        gall = sb("gall", [128, NC2, 16])
        th = sb("th", [128, NC2, 16])
        gk = []
        for c0 in range(0, NC2, 16):
            n = min(16, NC2 - c0)
            k = "gall%d" % (c0 // 16)
            gk.append(k)
            P.op("sp", lambda e, c0=c0, n=n: e.dma_start(
                out=gall[:, c0:c0 + n, :], in_=g_s[c0 * 128:(c0 + n) * 128, :].rearrange("(c p) j -> p c j", p=128)),
                r=[skey("g_s", b_) for b_ in range(c0 // 4, (c0 + n + 3) // 4)], w=[k], dma="gall")
        P.op("act", lambda e: e.activation(out=th[:], in_=gall[:], func=AF.Tanh, scale=1.0 / CAP), r=gk, w=["th"])
        dirs = {}
        for dn, ncd, co, tri in (("A", NCH, 8, Uf), ("B", NC2, 0, Lf)):
            W = ncd * 4
            IG = sb("IG" + dn, [128, ncd, 4])
            E = sb("E" + dn, [128, ncd, 4])
            LF = sb("LF" + dn, [128, ncd, 4])
            bsb = sb("b" + dn, [128, W])
            r_ = sb("r" + dn, [128, W])
            Rcol = sb("Rcol" + dn, [128, 1])
            Rrow = sb("Rrow" + dn, [1, W])
            bLrow = sb("bLrow" + dn, [1, W])
            Mrow = sb("Mrow" + dn, [1, W])
            larow = sb("larow" + dn, [1, W])
            mrow = sb("mrow" + dn, [1, 4])
            tmp = sb("tmp" + dn, [128, W])
            w_ = sb("w" + dn, [128, ncd, 4])
            cl = sb("cl" + dn, [128, ncd, 4])
            ab = sb("ab" + dn, [128, ncd, 4])
            k = lambda s_, dn=dn: s_ + dn
            P.op("dve", lambda e, IG=IG, ncd=ncd, co=co: e.tensor_scalar(out=IG[:], in0=th[:, 0:ncd, co:co + 4], scalar1=CAP,
                                                                        scalar2=None, op0=ALU.mult), r=["th"], w=[k("IG")])
            P.op("act", lambda e, E=E, ncd=ncd, co=co: e.activation(out=E[:], in_=th[:, 0:ncd, co + 4:co + 8], func=AF.Exp,
                                                                   scale=-CAP), r=["th"], w=[k("E")])
            P.op("act", lambda e, E=E: e.activation(out=E[:], in_=E[:], func=AF.Ln, bias=1.0), r=[k("E")], w=[k("E")])
            P.op("dve", lambda e, E=E, LF=LF: e.tensor_scalar(out=LF[:], in0=E[:], scalar1=-1.0, scalar2=None, op0=ALU.mult),
                 r=[k("E")], w=[k("LF")])
            LFf = LF[:].rearrange("p c h -> p (c h)")
            P.op("pe", lambda e, tri=tri, LFf=LFf, W=W: e.matmul(ps[0][:, 0:W], lhsT=tri, rhs=LFf, start=True, stop=True),
                 r=["cst_f", k("LF")], w=psk(0))
            P.op("pe", lambda e, LFf=LFf, W=W: e.matmul(ps[1][0:1, 0:W], lhsT=ones_f[:, 0:1], rhs=LFf, start=True, stop=True),
                 r=["ones_f", k("LF")], w=psk(1))
            P.op("act", lambda e, bsb=bsb, W=W: e.copy(out=bsb[:], in_=ps[0][:, 0:W]), r=psk(0), w=[k("b")])
            P.op("act", lambda e, bLrow=bLrow, W=W: e.copy(out=bLrow[:], in_=ps[1][0:1, 0:W]), r=psk(1), w=[k("rec")])
            P.op("dve", lambda e, r_=r_, IG=IG, bsb=bsb: e.tensor_tensor(out=r_[:], in0=IG[:].rearrange("p c h -> p (c h)"),
                                                                        in1=bsb[:], op=ALU.subtract),
                 r=[k("IG"), k("b")], w=[k("r")])
            for g0 in range(0, W, 128):
                gw = min(128, W - g0)
                P.op("pe", lambda e, r_=r_, g0=g0, gw=gw: e.transpose(out=ps[2][0:gw, 0:128], in_=r_[:, g0:g0 + gw],
                                                                     identity=identf), r=[k("r"), "cst_f"], w=psk(2))
                P.op("dve", lambda e, Rcol=Rcol, gw=gw: e.tensor_reduce(out=Rcol[0:gw, :], in_=ps[2][0:gw, 0:128], axis=AX.X,
                                                                       op=ALU.max), r=psk(2), w=[k("Rcol")])
                P.op("pe", lambda e, Rcol=Rcol, g0=g0, gw=gw: e.matmul(ps[3][0:1, g0:g0 + gw], lhsT=Rcol[0:gw, 0:1],
                                                                      rhs=cst_f[0:gw, 0:gw], start=True, stop=True),
                     r=[k("Rcol"), "cst_f"], w=psk(3))
            P.op("act", lambda e, Rrow=Rrow, W=W: e.copy(out=Rrow[:], in_=ps[3][0:1, 0:W]), r=psk(3), w=[k("rec")])
            P.op("dve", lambda e, mrow=mrow: e.memset(mrow[:], 0.0), w=[k("rec")])
            order = range(ncd) if dn == "A" else range(ncd - 1, -1, -1)
            for c in order:
                sl = slice(c * 4, c * 4 + 4)
                P.op("dve", lambda e, sl=sl, Mrow=Mrow, mrow=mrow, Rrow=Rrow: e.tensor_tensor(
                    out=Mrow[0:1, sl], in0=mrow[:], in1=Rrow[0:1, sl], op=ALU.max), r=[k("rec")], w=[k("rec")])
                P.op("dve", lambda e, sl=sl, Mrow=Mrow, mrow=mrow, larow=larow: e.tensor_tensor(
                    out=larow[0:1, sl], in0=mrow[:], in1=Mrow[0:1, sl], op=ALU.subtract), r=[k("rec")], w=[k("rec")])
                P.op("dve", lambda e, sl=sl, Mrow=Mrow, mrow=mrow, bLrow=bLrow: e.tensor_tensor(
                    out=mrow[:], in0=bLrow[0:1, sl], in1=Mrow[0:1, sl], op=ALU.add), r=[k("rec")], w=[k("rec")])
            P.op("pe", lambda e, Mrow=Mrow, W=W: e.matmul(ps[4][:, 0:W], lhsT=ones_f[0:1, :], rhs=Mrow[:], start=True, stop=True),
                 r=["ones_f", k("rec")], w=psk(4))
            P.op("pe", lambda e, larow=larow, W=W: e.matmul(ps[5][:, 0:W], lhsT=ones_f[0:1, :], rhs=larow[:], start=True, stop=True),
                 r=["ones_f", k("rec")], w=psk(5))
            P.op("dve", lambda e, tmp=tmp, r_=r_, W=W: e.tensor_tensor(out=tmp[:], in0=r_[:], in1=ps[4][:, 0:W], op=ALU.subtract),
                 r=[k("r")] + psk(4), w=[k("tmp")])
            P.op("act", lambda e, tmp=tmp, w_=w_: e.activation(out=w_[:].rearrange("p c h -> p (c h)"), in_=tmp[:], func=AF.Exp),
                 r=[k("tmp")], w=[k("w")])
            P.op("dve", lambda e, tmp=tmp, bsb=bsb, W=W: e.scalar_tensor_tensor(out=tmp[:], in0=bsb[:], scalar=-1.0,
                                                                               in1=ps[4][:, 0:W], op0=ALU.mult, op1=ALU.subtract),
                 r=[k("b"), k("w")] + psk(4), w=[k("tmp")])
            P.op("act", lambda e, tmp=tmp, cl=cl: e.activation(out=cl[:].rearrange("p c h -> p (c h)"), in_=tmp[:], func=AF.Exp),
                 r=[k("tmp")], w=[k("cl")])
            P.op("act", lambda e, ab=ab, W=W: e.activation(out=ab[:].rearrange("p c h -> p (c h)"), in_=ps[5][:, 0:W], func=AF.Exp),
                 r=psk(5), w=[k("ab")])
            dirs[dn] = dict(w=w_, cl=cl, ab=ab, ncd=ncd, mask=tri)

        kTc = [sb("kTc%d" % i, [128, MH, 128], BF16) for i in range(2)]
        qTc = [sb("qTc%d" % i, [128, MH, 128], BF16) for i in range(2)]
        Vc = [sb("Vc%d" % i, [128, MH, MDV], BF16) for i in range(2)]
        kp = [sb("kp%d" % i, [128, MH, 128], BF16) for i in range(2)]
        Spf = sb("Spf", [128, MH, 128])
        SpT = [sb("SpT%d" % i, [128, MH, 128], BF16) for i in range(2)]
        Cs = sb("Cs", [128, MH, MDV])
        ns = sb("ns", [128, MH])
        Cb = sb("Cb", [128, MH, MDV], BF16)
        nb_ = sb("nb_", [128, MH], BF16)
        dd = sb("dd", [128, MH])
        rec = sb("rec", [128, MH])
        hout = [sb("hout%d" % i, [128, MH, MDV]) for i in range(2)]
        hAc = [sb("hAc%d" % i, [128, MH * MDV]) for i in range(2)]
        moc = [sb("moc%d" % i, [128, MH * MDV]) for i in range(2)]
        mnw_b = sb("mnw_b", [128, MH * MDV])
        ssq = sb("ssq", [128, MH])
        sqj = sb("sqj", [128, MDV], BF16)
        hmg = sb("hmg", [128, MH * MDV], BF16)
        hmTc = [sb("hmTc%d" % i, [128, 8, 128], BF16) for i in range(2)]
        P.op("sp", lambda e: e.dma_start(out=mnw_b[:], in_=mnw_d.partition_broadcast(128)), w=["mnw_b", "once"], dma="once")

        steps = [("A", c, True) for c in range(NCH)] + [("B", c, c < NCH) for c in range(NC2 - 1, -1, -1)]

        def load_step(i):
            d, c, fl = steps[i]
            s = i % 2
            blk = c // 4
            P.op("sp", lambda e, s=s, c=c: e.dma_start(out=kTc[s][:], in_=mkT_s[:, :, c * 128:(c + 1) * 128]),
                 r=[skey("mkT_s", blk)], w=["kTc%d" % s], dma="kTc%d" % s)
            P.op("sp", lambda e, s=s, c=c: e.dma_start(out=Vc[s][:], in_=mV_s[c * 128:(c + 1) * 128, :].rearrange(
                "p (h v) -> p h v", h=MH)), r=[skey("mV_s", blk)], w=["Vc%d" % s], dma="Vc%d" % s)
            if fl:
                P.op("sp", lambda e, s=s, c=c: e.dma_start(out=qTc[s][:], in_=mqT_s[:, :, c * 128:(c + 1) * 128]),
                     r=[skey("mqT_s", blk)], w=["qTc%d" % s], dma="qTc%d" % s)

        def load_late(i):
            d, c, fl = steps[i]
            s = i % 2
            if fl and d == "B":
                P.op("sp", lambda e, s=s, c=c: e.dma_start(out=hAc[s][:], in_=hA_s[c * 128:(c + 1) * 128, :]),
                     r=[skey("hA_s", c)], w=["hAc%d" % s], dma="hAc%d" % s)
                P.op("sp", lambda e, s=s, c=c: e.dma_start(out=moc[s][:], in_=mo_s[c * 128:(c + 1) * 128, :]),
                     r=[skey("mo_s", c // 4)], w=["moc%d" % s], dma="moc%d" % s)

        load_step(0)
        for i, (d, c, fl) in enumerate(steps):
            s = i % 2
            dd_ = dirs[d]
            w_, cl, ab, mask = dd_["w"], dd_["cl"], dd_["ab"], dd_["mask"]
            first = (d == "A" and c == 0) or (d == "B" and c == NC2 - 1)
            if first:
                P.op("dve", lambda e: e.memset(Cs[:], 0.0), w=["Cs"])
                P.op("dve", lambda e: e.memset(ns[:], 0.0), w=["ns"])
            load_late(i)
            if i + 1 < len(steps):
                load_step(i + 1)
            def trk(e, s=s):
                ins = None
                pv = ps[0].bitcast(BF16)
                for h in range(MH):
                    ins = e.transpose(out=pv[:, h * 128:(h + 1) * 128], in_=kTc[s][:, h, :], identity=ident[:])
                return ins
            P.op("pe", trk, r=["kTc%d" % s, "ident"], w=psk(0))
            P.op("dve", lambda e, s=s, c=c, w_=w_: e.tensor_tensor(
                out=kp[s][:], in0=ps[0].bitcast(BF16)[:, 0:512].rearrange("p (h k) -> p h k", h=MH),
                in1=w_[:, c, :].unsqueeze(2).to_broadcast([128, MH, 128]), op=ALU.mult),
                r=psk(0) + ["w" + d], w=["kp%d" % s])
            P.op("dve", lambda e, c=c, ab=ab: e.tensor_tensor(out=Cs[:], in0=Cs[:], in1=ab[:, c, :].unsqueeze(2).to_broadcast(
                [128, MH, MDV]), op=ALU.mult), r=["Cs", "ab" + d], w=["Cs"])
            P.op("dve", lambda e, c=c, ab=ab: e.tensor_tensor(out=ns[:], in0=ns[:], in1=ab[:, c, :], op=ALU.mult),
                 r=["ns", "ab" + d], w=["ns"])
            if fl:
                P.op("act", lambda e: e.copy(out=Cb[:], in_=Cs[:]), r=["Cs"], w=["Cb"])
                P.op("act", lambda e: e.copy(out=nb_[:], in_=ns[:]), r=["ns"], w=["nb_"])

                def mms(e, s=s):
                    ins = None
                    for h in range(MH):
                        ins = e.matmul(ps[1][:, h * 128:(h + 1) * 128], lhsT=kTc[s][:, h, :], rhs=qTc[s][:, h, :],
                                       start=True, stop=True)
                    return ins
                P.op("pe", mms, r=["kTc%d" % s, "qTc%d" % s], w=psk(1))
                P.op("dve", lambda e, c=c, w_=w_: e.tensor_tensor(
                    out=Spf[:], in0=ps[1].rearrange("p (h k) -> p h k", h=MH),
                    in1=w_[:, c, :].unsqueeze(2).to_broadcast([128, MH, 128]), op=ALU.mult),
                    r=psk(1) + ["w" + d], w=["Spf"])
                P.op("dve", lambda e, s=s, mask=mask: e.tensor_tensor(
                    out=SpT[s][:], in0=Spf[:], in1=mask.unsqueeze(1).to_broadcast([128, MH, 128]), op=ALU.mult),
                    r=["Spf", "cst_f"], w=["SpT%d" % s])

                def mmn(e, s=s):
                    ins = None
                    for h in range(MH):
                        o = ps[2 + h // 2][:, (h % 2) * 256:(h % 2) * 256 + 256]
                        e.matmul(o, lhsT=qTc[s][:, h, :], rhs=Cb[:, h, :], start=True, stop=False)
                        e.matmul(o, lhsT=SpT[s][:, h, :], rhs=Vc[s][:, h, :], start=False, stop=True)
                    for h in range(MH):
                        o = ps[6][:, h:h + 1]
                        e.matmul(o, lhsT=qTc[s][:, h, :], rhs=nb_[:, h:h + 1], start=True, stop=False)
                        ins = e.matmul(o, lhsT=SpT[s][:, h, :], rhs=ones_b[:, 0:1], start=False, stop=True)
                    return ins
                P.op("pe", mmn, r=["qTc%d" % s, "Cb", "nb_", "SpT%d" % s, "Vc%d" % s, "ones_b"], w=psk(2, 3, 6))

            def mmc(e, s=s):
                ins = None
                for h in range(MH):
                    o = ps[4 + h // 2][:, (h % 2) * 256:(h % 2) * 256 + 256]
                    e.matmul(o, lhsT=kp[s][:, h, :], rhs=Vc[s][:, h, :], start=True, stop=True)
                for h in range(MH):
                    ins = e.matmul(ps[7][:, h:h + 1], lhsT=kp[s][:, h, :], rhs=ones_b[:, 0:1], start=True, stop=True)
                return ins
            P.op("pe", mmc, r=["kp%d" % s, "Vc%d" % s, "ones_b"], w=psk(4, 5, 7))
            if fl:
                P.op("act", lambda e: e.activation(out=dd[:], in_=ps[6][:, 0:MH], func=AF.Abs), r=psk(6), w=["dd"])
                P.op("dve", lambda e, c=c, cl=cl: e.tensor_tensor(out=dd[:], in0=dd[:], in1=cl[:, c, :], op=ALU.max),
                     r=["dd", "cl" + d], w=["dd"])
                P.op("dve", lambda e: e.reciprocal(out=rec[:], in_=dd[:]), r=["dd"], w=["rec"])
                for hh in range(2):
                    P.op("dve", lambda e, s=s, hh=hh: e.tensor_tensor(
                        out=hout[s][:, hh * 2:hh * 2 + 2, :], in0=ps[2 + hh].rearrange("p (h v) -> p h v", h=2),
                        in1=rec[:, hh * 2:hh * 2 + 2].unsqueeze(2).to_broadcast([128, 2, MDV]), op=ALU.mult),
                        r=psk(2 + hh) + ["rec"], w=["hout%d" % s])
            for hh in range(2):
                P.op("dve", lambda e, hh=hh: e.tensor_tensor(
                    out=Cs[:, hh * 2:hh * 2 + 2, :], in0=Cs[:, hh * 2:hh * 2 + 2, :],
                    in1=ps[4 + hh].rearrange("p (h v) -> p h v", h=2), op=ALU.add), r=psk(4 + hh) + ["Cs"], w=["Cs"])
            P.op("dve", lambda e: e.tensor_tensor(out=ns[:], in0=ns[:], in1=ps[7][:, 0:MH], op=ALU.add), r=psk(7) + ["ns"], w=["ns"])
            if fl and d == "A":
                P.op("sp", lambda e, s=s, c=c: e.dma_start(out=hA_s[c * 128:(c + 1) * 128, :],
                                                          in_=hout[s][:].rearrange("p h v -> p (h v)")),
                     r=["hout%d" % s], w=[skey("hA_s", c)], dma="hout%d" % s)
            if fl and d == "B":
                hf = hout[s][:].rearrange("p h v -> p (h v)")
                P.op("dve", lambda e, s=s, hf=hf: e.tensor_tensor(out=hf, in0=hf, in1=hAc[s][:], op=ALU.add),
                     r=["hout%d" % s, "hAc%d" % s], w=["hout%d" % s])
                for h in range(MH):
                    P.op("act", lambda e, s=s, h=h: e.activation(out=sqj[:], in_=hout[s][:, h, :], func=AF.Square,
                                                                accum_out=ssq[:, h:h + 1]),
                         r=["hout%d" % s], w=["sqj", "ssq"])
                P.op("dve", lambda e: e.tensor_scalar(out=ssq[:], in0=ssq[:], scalar1=1.0 / MDV, scalar2=EPS, op0=ALU.mult,
                                                      op1=ALU.add), r=["ssq"], w=["ssq"])
                P.op("act", lambda e: e.sqrt(out=ssq[:], in_=ssq[:]), r=["ssq"], w=["ssq"])
                P.op("dve", lambda e: e.reciprocal(out=ssq[:], in_=ssq[:]), r=["ssq"], w=["ssq"])
                for h in range(MH):
                    P.op("dve", lambda e, s=s, h=h: e.scalar_tensor_tensor(
                        out=hout[s][:, h, :], in0=hout[s][:, h, :], scalar=ssq[:, h:h + 1],
                        in1=mnw_b[:, h * MDV:(h + 1) * MDV], op0=ALU.mult, op1=ALU.mult),
                        r=["hout%d" % s, "ssq", "mnw_b"], w=["hout%d" % s])
                P.op("dve", lambda e, s=s, hf=hf: e.tensor_tensor(out=hmg[:], in0=hf, in1=moc[s][:], op=ALU.mult),
                     r=["hout%d" % s, "moc%d" % s], w=["hmg"])

                def trh(e):
                    ins = None
                    pv = ps[1].bitcast(BF16)
                    for cc in range(8):
                        ins = e.transpose(out=pv[:, cc * 128:(cc + 1) * 128], in_=hmg[:, cc * 128:(cc + 1) * 128],
                                          identity=ident[:])
                    return ins
                P.op("pe", trh, r=["hmg", "ident"], w=psk(1))
                P.op("act", lambda e, s=s: e.copy(out=hmTc[s][:], in_=ps[1].bitcast(BF16).rearrange("p (c t) -> p c t", c=8)),
                     r=psk(1), w=["hmTc%d" % s])
                P.op("sp", lambda e, s=s, c=c: e.dma_start(out=hmT_s[:, :, c * 128:(c + 1) * 128], in_=hmTc[s][:]),
                     r=["hmTc%d" % s], w=[skey("hmT_s", c // 4)], dma="hmTc%d" % s)
    P.barrier()
    if stop == "p3":
        return finish()

    anw_d = din("anw", [128, 1])
    lam_d = din("lamv", [1, 256])
    nlam = gsb("nlam", [128, 1])
    with ExitStack() as st:
        def sb(name, shape, dt=F32):
            return st.enter_context(nc.sbuf_tensor(name, list(shape), dt))
        lamr = sb("lamr", [1, 256])
        lprod = sb("lprod", [1, 128])
        ldot = sb("ldot", [1, 2])
        anws = sb("anws", [128, 1])
        P.op("sp", lambda e: e.dma_start(out=lamr[:], in_=lam_d), w=["lamr", "once"], dma="once")
        P.op("sp", lambda e: e.dma_start(out=anws[:], in_=anw_d), w=["anws", "once"], dma="once")
        P.op("dve", lambda e: e.tensor_tensor(out=lprod[:].rearrange("p (a k) -> p a k", a=2),
                                              in0=lamr[:].rearrange("p (a b k) -> p a b k", a=2, b=2)[:, :, 0, :],
                                              in1=lamr[:].rearrange("p (a b k) -> p a b k", a=2, b=2)[:, :, 1, :],
                                              op=ALU.mult), r=["lamr"], w=["lprod"])
        P.op("dve", lambda e: e.tensor_reduce(out=ldot[:], in_=lprod[:].rearrange("p (a k) -> p a k", a=2), axis=AX.X,
                                              op=ALU.add), r=["lprod"], w=["ldot"])
        P.op("act", lambda e: e.activation(out=ldot[:], in_=ldot[:], func=AF.Exp), r=["ldot"], w=["ldot"])
        P.op("dve", lambda e: e.tensor_tensor(out=lprod[0:1, 0:1], in0=ldot[0:1, 1:2], in1=ldot[0:1, 0:1], op=ALU.subtract),
             r=["ldot", "lprod"], w=["lprod"])
        P.op("dve", lambda e: e.tensor_scalar(out=lprod[0:1, 0:1], in0=lprod[0:1, 0:1], scalar1=-LAM_INIT, scalar2=None,
                                              op0=ALU.add), r=["lprod"], w=["lprod"])
        P.op("pe", lambda e: e.matmul(ps[0][:, 0:1], lhsT=ones_f[0:1, :], rhs=lprod[0:1, 0:1], start=True, stop=True),
             r=["ones_f", "lprod"], w=psk(0))
        P.op("act", lambda e: e.copy(out=nlam[:], in_=ps[0][:, 0:1]), r=psk(0), w=["nlam"])
        P.op("dve", lambda e: e.tensor_scalar(out=anws[:], in0=anws[:], scalar1=1.0 - LAM_INIT, scalar2=None, op0=ALU.mult),
             r=["anws"], w=["anws"])

        NKT = T2 // 128
        akT = [sb("akT%d" % i, [128, T2], BF16) for i in range(2)]
        aVh = [sb("aVh%d" % i, [128, NKT, ADV], BF16) for i in range(2)]
        aq = [sb("aq%d" % i, [128, 512], BF16) for i in range(2)]
        Pt = [sb("Pt%d" % i, [128, 1024], BF16) for i in range(2)]
        r0_ = sb("r0_", [128, 512])
        r1_ = sb("r1_", [128, 512])
        t0_ = sb("t0_", [128, 512])
        t1_ = sb("t1_", [128, 512])
        sq_ = sb("sq_", [128, 512])
        rstd_ = sb("rstd_", [128, 512])
        hao = [sb("hao%d" % i, [128, 512], BF16) for i in range(2)]
        SC = ADH ** -0.5
        nqb = 0
        for h in range(AH):
            hs = h % 2
            P.op("sp", lambda e, hs=hs, h=h: e.dma_start(out=akT[hs][:], in_=akT_s[:, h, :]),
                 r=[skey("akT_s", b_) for b_ in range(NB2)], w=["akT%d" % hs], dma="akT%d" % hs)
            for k0 in range(0, NKT, 16):
                kn = min(16, NKT - k0)
                P.op("sp", lambda e, hs=hs, h=h, k0=k0, kn=kn: e.dma_start(
                    out=aVh[hs][:, k0:k0 + kn, :],
                    in_=aV_s[k0 * 128:(k0 + kn) * 128, h * ADV:(h + 1) * ADV].rearrange("(k p) v -> p k v", p=128)),
                    r=[skey("aV_s", b_) for b_ in range(NB2)], w=["aVh%d_%d" % (hs, k0)], dma="aVh%d" % hs)
            avk = ["aVh%d_%d" % (hs, k0) for k0 in range(0, NKT, 16)]
            for qb in range(NB):
                qs_ = nqb % 2
                nqb += 1
                P.op("sp", lambda e, qs_=qs_, h=h, qb=qb: e.dma_start(out=aq[qs_][:], in_=aqT_s[:, h, qb * 512:(qb + 1) * 512]),
                     r=[skey("aqT_s", qb)], w=["aq%d" % qs_], dma="aq%d" % qs_)

                def qk(kt, hs=hs, qs_=qs_):
                    pi = kt % 2

                    def f(e):
                        e.matmul(pp[pi][:, 0:512], lhsT=akT[hs][0:64, kt * 128:(kt + 1) * 128], rhs=aq[qs_][0:64, :],
                                 start=True, stop=True)
                        return e.matmul(pp[pi][:, 512:1024], lhsT=akT[hs][64:128, kt * 128:(kt + 1) * 128],
                                        rhs=aq[qs_][64:128, :], start=True, stop=True)
                    P.op("pe", f, r=["akT%d" % hs, "aq%d" % qs_], w=psk(2 * pi, 2 * pi + 1))

                def pv(kt, hs=hs):
                    pi = kt % 2
                    P.op("act", lambda e: e.activation(out=Pt[pi][:], in_=pp[pi][:], func=AF.Exp, scale=SC),
                         r=psk(2 * pi, 2 * pi + 1), w=["Pt%d" % pi])

                    def f(e):
                        st_, sp_ = (kt == 0), (kt == NKT - 1)
                        e.matmul(ps[4], lhsT=aVh[hs][:, kt, :], rhs=Pt[pi][:, 0:512], start=st_, stop=sp_)
                        e.matmul(ps[5], lhsT=aVh[hs][:, kt, :], rhs=Pt[pi][:, 512:1024], start=st_, stop=sp_)
                        e.matmul(ps[6], lhsT=ones_b[:], rhs=Pt[pi][:, 0:512], start=st_, stop=sp_)
                        return e.matmul(ps[7], lhsT=ones_b[:], rhs=Pt[pi][:, 512:1024], start=st_, stop=sp_)
                    P.op("pe", f, r=avk + ["Pt%d" % pi, "ones_b"], w=psk(4, 5, 6, 7))
                qk(0)
                for kt in range(NKT):
                    if kt + 1 < NKT:
                        qk(kt + 1)
                    pv(kt)
                ho = nqb % 2
                P.op("dve", lambda e: e.reciprocal(out=r0_[:], in_=ps[6]), r=psk(6), w=["r0_"])
                P.op("dve", lambda e: e.reciprocal(out=r1_[:], in_=ps[7]), r=psk(7), w=["r1_"])
                P.op("dve", lambda e: e.tensor_tensor(out=t0_[:], in0=ps[4], in1=r0_[:], op=ALU.mult), r=psk(4) + ["r0_"], w=["t0_"])
                P.op("dve", lambda e: e.tensor_tensor(out=t1_[:], in0=ps[5], in1=r1_[:], op=ALU.mult), r=psk(5) + ["r1_"], w=["t1_"])
                P.op("dve", lambda e: e.scalar_tensor_tensor(out=t0_[:], in0=t1_[:], scalar=nlam[:, 0:1], in1=t0_[:],
                                                             op0=ALU.mult, op1=ALU.add), r=["t0_", "t1_", "nlam"], w=["t0_"])
                P.op("act", lambda e: e.activation(out=sq_[:], in_=t0_[:], func=AF.Square), r=["t0_"], w=["sq_"])
                P.op("pe", lambda e: e.matmul(ps[6], lhsT=ones_f[:], rhs=sq_[:], start=True, stop=True), r=["ones_f", "sq_"], w=psk(6))
                P.op("dve", lambda e: e.tensor_scalar(out=rstd_[:], in0=ps[6], scalar1=1.0 / ADV, scalar2=EPS, op0=ALU.mult,
                                                      op1=ALU.add), r=psk(6), w=["rstd_"])
                P.op("act", lambda e: e.sqrt(out=rstd_[:], in_=rstd_[:]), r=["rstd_"], w=["rstd_"])
                P.op("dve", lambda e: e.reciprocal(out=rstd_[:], in_=rstd_[:]), r=["rstd_"], w=["rstd_"])
                P.op("dve", lambda e, ho=ho: e.scalar_tensor_tensor(out=hao[ho][:], in0=t0_[:], scalar=anws[:, 0:1], in1=rstd_[:],
                                                                    op0=ALU.mult, op1=ALU.mult),
                     r=["t0_", "anws", "rstd_"], w=["hao%d" % ho])
                P.op("sp", lambda e, ho=ho, h=h, qb=qb: e.dma_start(out=haT_s[:, h, qb * 512:(qb + 1) * 512], in_=hao[ho][:]),
                     r=["hao%d" % ho], w=[skey("haT_s", qb) + ".%d" % h], dma="hao%d" % ho)
                final_keys.append(skey("haT_s", qb) + ".%d" % h)
    P.barrier()
    if stop == "p4":
        return finish()

    n2w_d = din("n2w", [128, KC])
    fnw_d = din("fnw", [1, D])
    bgt_d = din("bgt", [128, 32])
    with ExitStack() as st:
        def sb(name, shape, dt=F32):
            return st.enter_context(nc.sbuf_tensor(name, list(shape), dt))
        UU = sb("UU", [128, 16384], BF16)
        hT_b = UU[:, 0:8192].rearrange("p (k t) -> p k t", k=KC)
        hmT_b = UU[:, 8192:12288].rearrange("p (k t) -> p k t", k=8)
        haT_b = UU[:, 12288:16384].rearrange("p (k t) -> p k t", k=8)
        actT = [UU[:, g * 5632:(g + 1) * 5632].rearrange("p (k t) -> p k t", k=11) for g in range(2)]
        UK = ["UU0", "UU1", "UU2"]
        wbm_t = [sb("wbm_t%d" % i, [128, 8, 128], BF16) for i in range(2)]
        wba_t = [sb("wba_t%d" % i, [128, 8, 128], BF16) for i in range(2)]
        wgm_t = [sb("wgm_t%d" % i, [128, KC, 128], BF16) for i in range(2)]
        wga_t = [sb("wga_t%d" % i, [128, KC, 128], BF16) for i in range(2)]
        sgm = sb("sgm", [128, 512])
        sga = sb("sga", [128, 512])
        mT = sb("mT", [128, KC, 512], BF16)
        wo_t = [sb("wo_t%d" % i, [128, 4, 512], BF16) for i in range(4)]
        xacc = sb("xacc", [128, 4, D])
        xn2 = sb("xn2", [128, D], BF16)
        h2T = sb("h2T", [128, KC, 512], BF16)
        wfg_t = [sb("wfg_t%d" % i, [128, KC, 128], BF16) for i in range(2)]
        wfu_t = [sb("wfu_t%d" % i, [128, KC, 128], BF16) for i in range(2)]
        sl_ = [sb("sl_%d" % i, [128, 512]) for i in range(2)]
        wfo_t = [sb("wfo_t%d" % i, [128, 11, 512], BF16) for i in range(2)]
        fnw_b = sb("fnw_b", [128, D])
        n2w = sb("n2w_sb", [128, KC])
        bgt = sb("bgt_sb", [128, 32])
        ss5 = sb("ss5", [128, 1])
        rs5 = sb("rs5", [128, 1])
        P.op("sp", lambda e: e.dma_start(out=fnw_b[:], in_=fnw_d.partition_broadcast(128)), w=["fnw_b", "once"], dma="once")
        P.op("sp", lambda e: e.dma_start(out=n2w[:], in_=n2w_d), w=["n2w", "once"], dma="once")
        P.op("sp", lambda e: e.dma_start(out=bgt[:], in_=bgt_d), w=["bgt", "once"], dma="once")
        nw5, nwo, nwf, nwfo, nsl = [0], [0], [0], [0], [0]

        def rms_rstd(src_ap, junk_ap, srckeys, junkkeys):
            P.op("act", lambda e: e.activation(out=junk_ap, in_=src_ap, func=AF.Square, accum_out=ss5[:]),
                 r=srckeys, w=junkkeys + ["ss5"])
            P.op("dve", lambda e: e.tensor_scalar(out=rs5[:], in0=ss5[:], scalar1=1.0 / D, scalar2=EPS, op0=ALU.mult,
                                                  op1=ALU.add), r=["ss5"], w=["rs5"])
            P.op("act", lambda e: e.sqrt(out=rs5[:], in_=rs5[:]), r=["rs5"], w=["rs5"])
            P.op("dve", lambda e: e.reciprocal(out=rs5[:], in_=rs5[:]), r=["rs5"], w=["rs5"])

        for blk in range(NB):
            t0 = blk * 512
            P.op("sp", lambda e, t0=t0: e.dma_start(out=hT_b, in_=hT_s[:, :, t0:t0 + 512]),
                 r=[skey("hT_s", blk)], w=UK[0:2], dma="ld_hT")
            P.op("sp", lambda e, t0=t0: e.dma_start(out=hmT_b, in_=hmT_s[:, :, t0:t0 + 512]),
                 r=[skey("hmT_s", blk)], w=UK[1:3], dma="ld_hm")
            P.op("sp", lambda e, t0=t0: e.dma_start(out=haT_b, in_=haT_s[:, :, t0:t0 + 512]),
                 r=[skey("haT_s", blk) + ".%d" % h for h in range(AH)], w=UK[2:3], dma="ld_ha")
            for sub in range(4):
                P.op("sp", lambda e, t0=t0, sub=sub: e.dma_start(out=xacc[:, sub, :], in_=x_d.rows(t0 + sub * 128, t0 + (sub + 1) * 128)),
                     w=["xacc%d" % sub], dma="xacc%d" % sub)
            for j in range(16):
                ws = nw5[0] % 2
                nw5[0] += 1
                P.op("sp", lambda e, ws=ws, j=j: e.dma_start(out=wbm_t[ws][:], in_=wb_bm[j]),
                     r=wkeys["wb_bm%d" % j], w=["wbm_t%d" % ws], dma="w5_%d" % ws)
                P.op("sp", lambda e, ws=ws, j=j: e.dma_start(out=wba_t[ws][:], in_=wb_ba[j]),
                     r=wkeys["wb_ba%d" % j], w=["wba_t%d" % ws], dma="w5_%d" % ws)
                P.op("sp", lambda e, ws=ws, j=j: e.dma_start(out=wgm_t[ws][:], in_=wb_gt[j]),
                     r=wkeys["wb_gt%d" % j], w=["wgm_t%d" % ws], dma="w5_%d" % ws)
                P.op("sp", lambda e, ws=ws, j=j: e.dma_start(out=wga_t[ws][:], in_=wb_gt[16 + j]),
                     r=wkeys["wb_gt%d" % (16 + j)], w=["wga_t%d" % ws], dma="w5_%d" % ws)
                pb = 4 * (j % 2)

                def mm5(e, ws=ws, pb=pb):
                    ins = None
                    for kc in range(8):
                        e.matmul(ps[pb], lhsT=wbm_t[ws][:, kc, :], rhs=hmT_b[:, kc, :], start=(kc == 0), stop=(kc == 7))
                    for kc in range(8):
                        e.matmul(ps[pb + 1], lhsT=wba_t[ws][:, kc, :], rhs=haT_b[:, kc, :], start=(kc == 0), stop=(kc == 7))
                    for kc in range(KC):
                        e.matmul(ps[pb + 2], lhsT=wgm_t[ws][:, kc, :], rhs=hT_b[:, kc, :], start=(kc == 0), stop=(kc == KC - 1))
                    for kc in range(KC):
                        ins = e.matmul(ps[pb + 3], lhsT=wga_t[ws][:, kc, :], rhs=hT_b[:, kc, :], start=(kc == 0),
                                       stop=(kc == KC - 1))
                    return ins
                P.op("pe", mm5, r=["wbm_t%d" % ws, "wba_t%d" % ws, "wgm_t%d" % ws, "wga_t%d" % ws] + UK,
                     w=psk(pb, pb + 1, pb + 2, pb + 3))
                P.op("act", lambda e, pb=pb, j=j: e.activation(out=sgm[:], in_=ps[pb + 2], func=AF.Sigmoid, bias=bgt[:, j:j + 1]),
                     r=psk(pb + 2) + ["bgt"], w=["sgm"])
                P.op("act", lambda e, pb=pb, j=j: e.activation(out=sga[:], in_=ps[pb + 3], func=AF.Sigmoid,
                                                              bias=bgt[:, 16 + j:17 + j]), r=psk(pb + 3) + ["bgt"], w=["sga"])
                P.op("dve", lambda e, pb=pb: e.tensor_tensor(out=sgm[:], in0=sgm[:], in1=ps[pb], op=ALU.mult),
                     r=psk(pb) + ["sgm"], w=["sgm"])
                P.op("dve", lambda e, pb=pb: e.tensor_tensor(out=sga[:], in0=sga[:], in1=ps[pb + 1], op=ALU.mult),
                     r=psk(pb + 1) + ["sga"], w=["sga"])
                P.op("dve", lambda e, j=j: e.tensor_tensor(out=mT[:, j, :], in0=sgm[:], in1=sga[:], op=ALU.add),
                     r=["sgm", "sga"], w=["mT"])
            for nb in range(4):
                pb = 4 * (nb % 2)
                for kg in range(4):
                    wsl = nwo[0] % 4
                    nwo[0] += 1
                    P.op("sp", lambda e, wsl=wsl, nb=nb, kg=kg: e.dma_start(out=wo_t[wsl][:], in_=wb_out[nb][:, kg * 4:(kg + 1) * 4, :]),
                         r=wkeys["wb_out%d" % nb], w=["wo_t%d" % wsl], dma="wo_t%d" % wsl)

                    def mmo(e, wsl=wsl, kg=kg, pb=pb):
                        ins = None
                        for sub in range(4):
                            for kl in range(4):
                                kc = kg * 4 + kl
                                ins = e.matmul(ps[pb + sub], lhsT=mT[:, kc, sub * 128:(sub + 1) * 128], rhs=wo_t[wsl][:, kl, :],
                                               start=(kc == 0), stop=(kc == KC - 1))
                        return ins
                    P.op("pe", mmo, r=["mT", "wo_t%d" % wsl], w=psk(pb, pb + 1, pb + 2, pb + 3))
                for sub in range(4):
                    P.op("dve", lambda e, sub=sub, nb=nb, pb=pb: e.tensor_tensor(
                        out=xacc[:, sub, nb * 512:(nb + 1) * 512], in0=xacc[:, sub, nb * 512:(nb + 1) * 512],
                        in1=ps[pb + sub], op=ALU.add), r=psk(pb + sub) + ["xacc%d" % sub], w=["xacc%d" % sub])
            for sub in range(4):
                rms_rstd(xacc[:, sub, :], xn2[:], ["xacc%d" % sub], ["xn2"])
                P.op("dve", lambda e, sub=sub: e.tensor_scalar(out=xn2[:], in0=xacc[:, sub, :], scalar1=rs5[:], scalar2=None,
                                                              op0=ALU.mult), r=["xacc%d" % sub, "rs5"], w=["xn2"])
                for half in range(2):
                    def tr2(e, half=half):
                        ins = None
                        pbv = ps[half].bitcast(BF16)
                        for c in range(8):
                            cc = half * 8 + c
                            ins = e.transpose(out=pbv[:, c * 128:(c + 1) * 128], in_=xn2[:, cc * 128:(cc + 1) * 128],
                                              identity=ident[:])
                        return ins
                    P.op("pe", tr2, r=["xn2", "ident"], w=psk(half))
                    P.op("dve", lambda e, half=half, sub=sub: e.tensor_tensor(
                        out=h2T[:, half * 8:(half + 1) * 8, sub * 128:(sub + 1) * 128],
                        in0=ps[half].bitcast(BF16).rearrange("p (c t) -> p c t", c=8),
                        in1=n2w[:, half * 8:(half + 1) * 8].unsqueeze(2).to_broadcast([128, 8, 128]), op=ALU.mult),
                        r=psk(half) + ["n2w"], w=["h2T"])
            for g in range(4):
                ga = g % 2
                ak = UK[0:1] if ga == 0 else UK[1:2]
                for jj in range(11):
                    j = g * 11 + jj
                    ws = nwf[0] % 2
                    nwf[0] += 1
                    P.op("sp", lambda e, ws=ws, j=j: e.dma_start(out=wfg_t[ws][:], in_=wb_fi[j]),
                         r=wkeys["wb_fi%d" % j], w=["wfg_t%d" % ws], dma="wf_%d" % ws)
                    P.op("sp", lambda e, ws=ws, j=j: e.dma_start(out=wfu_t[ws][:], in_=wb_fi[FC + j]),
                         r=wkeys["wb_fi%d" % (FC + j)], w=["wfu_t%d" % ws], dma="wf_%d" % ws)
                    pb = 2 * (jj % 2)

                    def mmf(e, ws=ws, pb=pb):
                        ins = None
                        for kc in range(KC):
                            e.matmul(ps[pb], lhsT=wfg_t[ws][:, kc, :], rhs=h2T[:, kc, :], start=(kc == 0), stop=(kc == KC - 1))
                        for kc in range(KC):
                            ins = e.matmul(ps[pb + 1], lhsT=wfu_t[ws][:, kc, :], rhs=h2T[:, kc, :], start=(kc == 0),
                                           stop=(kc == KC - 1))
                        return ins
                    P.op("pe", mmf, r=["wfg_t%d" % ws, "wfu_t%d" % ws, "h2T"], w=psk(pb, pb + 1))
                    si = nsl[0] % 2
                    nsl[0] += 1
                    P.op("act", lambda e, si=si, pb=pb: e.activation(out=sl_[si][:], in_=ps[pb], func=AF.Silu),
                         r=psk(pb), w=["sl_%d" % si])
                    P.op("dve", lambda e, si=si, pb=pb, ga=ga, jj=jj: e.tensor_tensor(out=actT[ga][:, jj, :], in0=sl_[si][:],
                                                                                     in1=ps[pb + 1], op=ALU.mult),
                         r=psk(pb + 1) + ["sl_%d" % si], w=ak)
                for nb in range(4):
                    ws = nwfo[0] % 2
                    nwfo[0] += 1
                    P.op("sp", lambda e, ws=ws, nb=nb, g=g: e.dma_start(out=wfo_t[ws][:], in_=wb_fo[nb, g]),
                         r=wkeys["wb_fo%d_%d" % (nb, g)], w=["wfo_t%d" % ws], dma="wfo_t%d" % ws)
                    for sub in range(4):
                        def mmo2(e, ws=ws, sub=sub, ga=ga):
                            ins = None
                            for kk in range(11):
                                ins = e.matmul(ps[4 + sub], lhsT=actT[ga][:, kk, sub * 128:(sub + 1) * 128], rhs=wfo_t[ws][:, kk, :],
                                               start=(kk == 0), stop=(kk == 10))
                            return ins
                        P.op("pe", mmo2, r=ak + ["wfo_t%d" % ws], w=psk(4 + sub))
                        P.op("dve", lambda e, sub=sub, nb=nb: e.tensor_tensor(
                            out=xacc[:, sub, nb * 512:(nb + 1) * 512], in0=xacc[:, sub, nb * 512:(nb + 1) * 512],
                            in1=ps[4 + sub], op=ALU.add), r=psk(4 + sub) + ["xacc%d" % sub], w=["xacc%d" % sub])
            for sub in range(4):
                rms_rstd(xacc[:, sub, :], xn2[:], ["xacc%d" % sub], ["xn2"])
                P.op("dve", lambda e, sub=sub: e.scalar_tensor_tensor(out=xacc[:, sub, :], in0=xacc[:, sub, :], scalar=rs5[:, 0:1],
                                                                     in1=fnw_b[:], op0=ALU.mult, op1=ALU.mult),
                     r=["xacc%d" % sub, "rs5", "fnw_b"], w=["xacc%d" % sub])
                ok = "out:%d:%d" % (blk, sub)
                final_keys.append(ok)
                P.op("sp", lambda e, sub=sub, t0=t0: e.dma_start(out=out_d[t0 + sub * 128:t0 + (sub + 1) * 128, :], in_=xacc[:, sub, :]),
                     r=["xacc%d" % sub], w=[ok], dma="xacc_st%d" % sub)
    return finish()


def _tiles_stat(W, cols, kc):
    out = np.empty((len(cols), 128, kc, 128), np.float32)
    for i, c0 in enumerate(cols):
        out[i] = W[:, c0:c0 + 128].reshape(kc, 128, 128).transpose(1, 0, 2)
    return out


def _tiles_mov(W, kc):
    return np.ascontiguousarray(W.reshape(kc, 128, W.shape[1]).transpose(1, 0, 2))


def prep_shared(inp):
    w_in = np.asarray(inp["w_in"][0], np.float32)
    sh = {}
    fm_cols = [O_MQ + h * 128 for h in range(4)] + [O_MK + h * 128 for h in range(4)] + \
              [O_AQ + h * 128 for h in range(8)] + [O_AK + h * 128 for h in range(8)]
    sh["win_fm"] = _tiles_stat(w_in, fm_cols, KC)
    sh["win_gt"] = _tiles_stat(w_in, [O_GT + j * 128 for j in range(32)], KC)
    sh["wbm"] = _tiles_stat(np.asarray(inp["w_branch_m"][0], np.float32), [j * 128 for j in range(16)], 8)
    sh["wba"] = _tiles_stat(np.asarray(inp["w_branch_a"][0], np.float32), [j * 128 for j in range(16)], 8)
    wout = np.asarray(inp["w_out"][0], np.float32)
    sh["wout"] = np.stack([_tiles_mov(wout[:, n * 512:(n + 1) * 512], KC) for n in range(4)])
    wfi = np.asarray(inp["w_ffn_in"][0], np.float32)
    sh["wfi"] = _tiles_stat(wfi, [j * 128 for j in range(2 * FC)], KC)
    wfo = np.asarray(inp["w_ffn_out"][0], np.float32)
    t = wfo.reshape(4, 11, 128, 4, 512)
    sh["wfo"] = np.ascontiguousarray(t.transpose(3, 0, 2, 1, 4))
    sh["n1w"] = np.ascontiguousarray(np.asarray(inp["norm1_w"][0], np.float32).reshape(KC, 128).T)
    sh["n2w"] = np.ascontiguousarray(np.asarray(inp["norm2_w"][0], np.float32).reshape(KC, 128).T)
    sh["fnw"] = np.asarray(inp["final_norm_w"], np.float32).reshape(1, D)
    sh["mnw"] = np.asarray(inp["mlstm_norm_w"][0], np.float32).reshape(1, MH * MDV)
    sh["anw"] = np.asarray(inp["attn_norm_w"][0], np.float32).reshape(128, 1)
    sh["bgt"] = np.ascontiguousarray(np.asarray(inp["b_branch_gate"][0], np.float32).reshape(32, 128).T)
    sh["lamv"] = np.concatenate([np.asarray(inp[k][0], np.float32) for k in ("lam_q1", "lam_k1", "lam_q2", "lam_k2")]).reshape(1, 256)
    ident = np.eye(128, dtype=np.float32)
    perm = np.zeros((128, 128), np.float32)
    for m in range(128):
        pm = m + 32 if (m % 64) < 32 else m - 32
        perm[pm, m] = 1.0
    U = np.triu(np.ones((128, 128), np.float32))
    L = np.tril(np.ones((128, 128), np.float32))
    sh["cst"] = np.concatenate([ident, perm, U, L], axis=1)
    sh["_w_in"] = w_in
    return sh


def prep_core(inp, sh, b, half, S):
    x = np.asarray(inp["x"][b], np.float32)
    pos = np.arange(S, dtype=np.float32)
    if half == 1:
        x = x[::-1]
        pos = pos[::-1]
    dA, dB = (0, 1) if half == 0 else (1, 0)
    w_in = sh["_w_in"]
    gcols = [O_MG + dB * 8 + k * 4 + h for k in range(2) for h in range(4)] + \
            [O_MG + dA * 8 + k * 4 + h for k in range(2) for h in range(4)]
    wtm = np.concatenate([w_in[:, O_MV:O_MV + 1024], w_in[:, O_AV:O_AV + 1024], w_in[:, O_MO:O_MO + 1024],
                          w_in[:, gcols]], axis=1)
    bi = np.asarray(inp["b_igate"][0], np.float32)
    bf = np.asarray(inp["b_fgate"][0], np.float32)
    gb = np.concatenate([bi[dB], bf[dB], bi[dA], bf[dA]]).reshape(1, 16)
    inv = (ROPE_THETA ** (-np.arange(0, ADH, 2, dtype=np.float32) / ADH)).astype(np.float32)
    ang = (pos[:, None] * inv[None, :]).astype(np.float32)
    cos = np.cos(ang).astype(np.float32)
    sin = np.sin(ang).astype(np.float32)
    pi = np.arange(128)
    ropec = np.ascontiguousarray(cos[:, pi % 32].T)
    sgn = np.where((pi % 64) < 32, -1.0, 1.0).astype(np.float32)
    ropes = np.ascontiguousarray((sin[:, pi % 32] * sgn[None, :]).T)
    m = {k: v for k, v in sh.items() if not k.startswith("_")}
    m["x"] = np.ascontiguousarray(x)
    m["win_tm"] = np.stack([_tiles_mov(wtm[:, n * 512:(n + 1) * 512], KC) for n in range(6)])
    m["win_tg"] = _tiles_mov(wtm[:, 3072:3088], KC)
    m["gbias"] = gb
    m["ropec"] = ropec
    m["ropes"] = ropes
    return split_map(m)


def split_map(m):
    out = {}
    for k, v in m.items():
        if k in SPLIT_PER:
            per = SPLIT_PER[k]
            for i in range(v.shape[0] // per):
                out["%s_%d" % (k, i)] = np.ascontiguousarray(v[i * per:(i + 1) * per])
        else:
            out[k] = v
    return out


N_CORES = 4


def kernel(**inputs):
    x = np.asarray(inputs["x"])
    B, S, _ = x.shape
    sh = prep_shared(inputs)
    if N_CORES == 8:
        T2, T = S, S // 2
        cores = [(b, h) for b in range(B) for h in range(2)]
    else:
        T2, T = S, S
        cores = [(b, 0) for b in range(B)]
    nc, P = build(T2, T)
    maps = [prep_core(inputs, sh, b, h, S) for (b, h) in cores]
    maps = [{k: v for k, v in m.items() if k in P.used_inputs} for m in maps]
    res = run_bass_kernel_spmd(nc, maps, core_ids=list(range(len(cores))))
    out = np.empty((B, S, D), np.float32)
    for i, (b, h) in enumerate(cores):
        o = np.asarray(res.results[i]["out"], np.float32)
        if N_CORES == 8:
            if h == 0:
                out[b, :T] = o
            else:
                out[b, T:] = o[::-1]
        else:
            out[b] = o
    return out
```

```python
import math
import bisect
from contextlib import ExitStack
import numpy as np
import concourse.bass as bass
import concourse.mybir as mybir
from concourse.bass_utils import run_bass_kernel_spmd

F32 = mybir.dt.float32
BF16 = mybir.dt.bfloat16
AF = mybir.ActivationFunctionType
ALU = mybir.AluOpType
AX = mybir.AxisListType

D = 2048
KC = D // 128
MH, MDK, MDV = 4, 128, 256
AH, ADH, ADV = 8, 64, 128
DFF = 5632
FC = DFF // 128
EPS = 1e-6
CAP = 15.0
LAM_INIT = 0.8 - 0.6 * math.exp(0.0)
ROPE_THETA = 10000.0
SEQ = 8192
O_MQ, O_MK, O_MV, O_MO, O_MG, O_AQ, O_AK, O_AV, O_GT = 0, 512, 1024, 2048, 3072, 3088, 4112, 5136, 6160
N_FM = 24
SPLIT_PER = {"x": 1024, "win_fm": 6, "win_tm": 3, "win_gt": 8, "wbm": 16, "wba": 16, "wout": 2, "wfi": 8, "wfo": 1}
TMW = 3088


class Prog:
    def __init__(self, nc):
        self.nc = nc
        self.ops = []

    def op(self, eng, fn, r=(), w=(), dma=None):
        w = tuple(w) + tuple(k for k in r if k.startswith("ps") and k not in w)
        self.ops.append(dict(eng=eng, fn=fn, r=tuple(r), w=w, dma=dma))

    def barrier(self):
        for e in ("pe", "act", "dve", "pool", "sp"):
            self.ops.append(dict(eng=e, fn=None, r=(), w=(), dma=None, bar=True))

    def emit(self):
        nc = self.nc
        ops = self.ops
        writers, readers = {}, {}
        deps = []
        last_eng, last_dma = {}, {}
        for i, o in enumerate(ops):
            d = {}
            if o.get("bar"):
                for j in list(last_eng.values()) + list(last_dma.values()):
                    d[j] = "raw"
            elif o["dma"] is not None:
                last_dma[o["dma"]] = i
            elif o["fn"] is not None:
                last_eng[o["eng"]] = i
            for k in o["r"]:
                for j in writers.get(k, ()):
                    d[j] = "raw"
            for k in o["w"]:
                rl = readers.get(k, [])
                wl = writers.get(k, [])
                for j in wl:
                    d.setdefault(j, "waw")
                for j in rl:
                    d.setdefault(j, "war")
            for k in o["r"]:
                readers.setdefault(k, []).append(i)
            for k in o["w"]:
                if readers.get(k):
                    writers[k] = [i]
                    readers[k] = []
                else:
                    writers.setdefault(k, []).append(i)
            dd = []
            for j, kind in d.items():
                if j == i:
                    continue
                p = ops[j]
                if o.get("bar"):
                    if p["dma"] is None and p["eng"] == o["eng"]:
                        continue
                    dd.append(j)
                    continue
                if p["dma"] is None and o["dma"] is None and p["eng"] == o["eng"] and o["eng"] == "pe":
                    continue
                dd.append(j)
            deps.append(dd)
        signaled = [False] * len(ops)
        for dd in deps:
            for j in dd:
                signaled[j] = True
        engs = ["pe", "act", "dve", "pool", "sp"]
        sems = {e: nc.alloc_semaphore("c_" + e) for e in engs}
        cnt = {e: 0 for e in engs}
        dcnt = {}
        ev = [None] * len(ops)
        for i, o in enumerate(ops):
            if o["dma"] is not None:
                k = o["dma"]
                if k not in sems:
                    sems[k] = nc.alloc_semaphore("d_" + k)
                    dcnt[k] = 0
                dcnt[k] += 16
                ev[i] = (k, dcnt[k])
            elif signaled[i] and o["fn"] is not None:
                cnt[o["eng"]] += 1
                ev[i] = (o["eng"], cnt[o["eng"]])
        per = {e: [] for e in engs}
        dma_hist = {}
        for i, o in enumerate(ops):
            per[o["eng"]].append(i)
            if o["dma"] is not None:
                h = dma_hist.setdefault(o["dma"], ([], []))
                h[0].append(i)
                h[1].append(ev[i][1])
        self.stats = {e: len(per[e]) for e in engs}
        self.nsem = len(sems)

        def run(e, name):
            waited = {}
            for i in per[name]:
                o = ops[i]
                need = {}
                for j in deps[i]:
                    s, v = ev[j]
                    if ops[j]["dma"] is not None:
                        idxs, vals = dma_hist[s]
                        v = vals[bisect.bisect_left(idxs, i) - 1]
                    if v > need.get(s, 0):
                        need[s] = v
                for s, v in need.items():
                    if v > waited.get(s, 0):
                        e.wait_ge(sems[s], v)
                        waited[s] = v
                if o["fn"] is None:
                    continue
                ins = o["fn"](e)
                if o["dma"] is not None:
                    ins.then_inc(sems[o["dma"]], 16)
                elif signaled[i]:
                    ins.then_inc(sems[name], 1)

        with nc.Block() as block:
            @block.tensor
            def _(e):
                run(e, "pe")

            @block.scalar
            def _(e):
                run(e, "act")

            @block.vector
            def _(e):
                run(e, "dve")

            @block.gpsimd
            def _(e):
                run(e, "pool")

            @block.sync
            def _(e):
                run(e, "sp")


def build(T2, T=None, dbg=(), stop=None):
    if T is None:
        T = T2 // 2
    full = (T == T2)
    NB2, NB = T2 // 512, T // 512
    NC2, NCH = T2 // 128, T // 128
    nc = bass.Bass("TRN2", target_bir_lowering=False)
    P = Prog(nc)
    used_inputs = []

    def din(name, shape, dt=F32):
        used_inputs.append(name)
        return nc.dram_tensor(name, list(shape), dt, kind="ExternalInput").ap()

    class SplitAP:
        def __init__(self, name, shape, per):
            self.per = per
            n = shape[0] // per
            self.pieces = [din("%s_%d" % (name, k), [per] + list(shape[1:])) for k in range(n)]

        def __getitem__(self, idx):
            if isinstance(idx, tuple):
                return self.pieces[idx[0] // self.per][(idx[0] % self.per,) + tuple(idx[1:])]
            return self.pieces[idx // self.per][idx % self.per]

        def rows(self, r0, r1):
            k = r0 // self.per
            assert (r1 - 1) // self.per == k
            return self.pieces[k][r0 - k * self.per:r1 - k * self.per, :]

    def dscr(name, shape, dt):
        kind = "ExternalOutput" if name in dbg else "Internal"
        return nc.dram_tensor(name, list(shape), dt, kind=kind).ap()

    def gsb(name, shape, dt=F32):
        return nc.alloc_sbuf_tensor(name, list(shape), dt)

    final_keys = []
    wkeys = {}

    def skey(name, blk):
        k = "%s:%d" % (name, blk)
        if k not in final_keys:
            final_keys.append(k)
        return k

    def finish():
        P.op("sp", None, r=list(final_keys))
        P.emit()
        P.used_inputs = used_inputs
        return nc, P

    x_d = SplitAP("x", [T2, D], SPLIT_PER["x"])
    win_fm_d = SplitAP("win_fm", [N_FM, 128, KC, 128], SPLIT_PER["win_fm"])
    win_tm_d = SplitAP("win_tm", [6, 128, KC, 512], SPLIT_PER["win_tm"])
    win_tg_d = din("win_tg", [128, KC, 16])
    n1w_d = din("n1w", [128, KC])
    gbias_d = din("gbias", [1, 16])
    ropec_d = din("ropec", [128, T2])
    ropes_d = din("ropes", [128, T2])
    cst_d = din("cst", [128, 4 * 128])
    out_d = nc.dram_tensor("out", [T, D], F32, kind="ExternalOutput").ap()

    wb_fm = dscr("wb_fm", [N_FM, 128, KC, 128], BF16)
    wb_tm = dscr("wb_tm", [6, 128, KC, 512], BF16)
    wb_tg = dscr("wb_tg", [128, KC, 16], BF16)
    hT_s = dscr("hT_s", [128, KC, T], BF16)
    mqT_s = dscr("mqT_s", [128, MH, T], BF16)
    mkT_s = dscr("mkT_s", [128, MH, T2], BF16)
    mV_s = dscr("mV_s", [T2, MH * MDV], BF16)
    mo_s = dscr("mo_s", [T, MH * MDV], F32)
    g_s = dscr("g_s", [T2, 16], F32)
    aqT_s = dscr("aqT_s", [128, AH, T], BF16)
    akT_s = dscr("akT_s", [128, AH, T2], BF16)
    aV_s = dscr("aV_s", [T2, AH * ADV], BF16)
    hA_s = dscr("hA_s", [T, MH * MDV], F32)
    hmT_s = dscr("hmT_s", [128, 8, T], BF16)
    haT_s = dscr("haT_s", [128, 8, T], BF16)

    cst_f = gsb("cst_f", [128, 512])
    ident = gsb("ident", [128, 128], BF16)
    perm = gsb("perm", [128, 128], BF16)
    ones_b = gsb("ones_b", [128, 128], BF16)
    ones_f = gsb("ones_f", [128, 128])
    n1w = gsb("n1w_sb", [128, KC])
    gbias = gsb("gbias_sb", [128, 16])
    identf = cst_f[:, 0:128]
    Uf = cst_f[:, 256:384]
    Lf = cst_f[:, 384:512]
    P.op("sp", lambda e: e.dma_start(out=cst_f[:], in_=cst_d), w=["cst_f", "once"], dma="once")
    P.op("sp", lambda e: e.dma_start(out=n1w[:], in_=n1w_d), w=["n1w", "once"], dma="once")
    P.op("sp", lambda e: e.dma_start(out=gbias[:], in_=gbias_d.partition_broadcast(128)), w=["gbias", "once"], dma="once")
    P.op("dve", lambda e: e.tensor_copy(out=ident[:], in_=cst_f[:, 0:128]), r=["cst_f"], w=["ident"])
    P.op("dve", lambda e: e.tensor_copy(out=perm[:], in_=cst_f[:, 128:256]), r=["cst_f"], w=["perm"])
    P.op("dve", lambda e: e.memset(ones_b[:], 1.0), w=["ones_b"])
    P.op("dve", lambda e: e.memset(ones_f[:], 1.0), w=["ones_f"])

    pp = [nc.alloc_psum_tensor("pp%d" % i, [128, 1024], F32) for i in range(4)]
    ps = []
    for i in range(4):
        ps.append(pp[i][:, 0:512])
        ps.append(pp[i][:, 512:1024])

    def psk(*idx):
        return ["ps%d" % i for i in idx]

    cvst = ExitStack()
    cv_in = [cvst.enter_context(nc.sbuf_tensor("cv_in%d" % i, [128, 2048], F32)) for i in range(2)]
    cv_out = [cvst.enter_context(nc.sbuf_tensor("cv_out%d" % i, [128, 2048], BF16)) for i in range(2)]
    cvn = [0]

    conv_jobs = []
    conv_pos = [0, 0]

    def convert_plan(src, dst, key0):
        Fdim = src.shape[1]
        wkeys[key0] = []
        for c0 in range(0, Fdim, 2048):
            key = "%s:%d" % (key0, c0 // 2048)
            wkeys[key0].append(key)
            final_keys.append(key)
            conv_jobs.append((src, dst, c0, min(2048, Fdim - c0), key))

    def conv_step():
        t = conv_pos[0]
        if t < len(conv_jobs):
            src, dst, c0, cw, key = conv_jobs[t]
            s = t % 2
            P.op("pool", lambda e, s=s, c0=c0, cw=cw, src=src: e.dma_start(out=cv_in[s][:, 0:cw], in_=src[:, c0:c0 + cw]),
                 w=["cv_in%d" % s], dma="cv_in%d" % s)
            conv_pos[0] += 1
        u = conv_pos[1]
        if u < conv_pos[0] - 1 or (conv_pos[0] == len(conv_jobs) and u < len(conv_jobs)):
            src, dst, c0, cw, key = conv_jobs[u]
            s = u % 2
            P.op("pool", lambda e, s=s, cw=cw: e.tensor_copy(out=cv_out[s][:, 0:cw], in_=cv_in[s][:, 0:cw]),
                 r=["cv_in%d" % s], w=["cv_out%d" % s])
            P.op("pool", lambda e, s=s, c0=c0, cw=cw, dst=dst: e.dma_start(out=dst[:, c0:c0 + cw], in_=cv_out[s][:, 0:cw]),
                 r=["cv_out%d" % s], w=[key], dma="cv_out%d" % s)
            conv_pos[1] += 1

    def conv_until(njobs):
        while conv_pos[1] < min(njobs, len(conv_jobs)):
            conv_step()

    def flat(ap):
        names = "abcdefg"[:len(ap.shape) - 1]
        return ap.rearrange("p %s -> p (%s)" % (" ".join(names), " ".join(names)))

    for j in range(N_FM):
        convert_plan(flat(win_fm_d[j]), flat(wb_fm[j]), "wb_fm%d" % j)
    for n in range(6):
        convert_plan(flat(win_tm_d[n]), flat(wb_tm[n]), "wb_tm%d" % n)
    convert_plan(flat(win_tg_d), flat(wb_tg), "wb_tg")
    n_upfront = len(conv_jobs)

    if stop is None or stop in ("p3", "p4", "p5"):
        win_gt_d = SplitAP("win_gt", [32, 128, KC, 128], SPLIT_PER["win_gt"])
        wbm_d = SplitAP("wbm", [16, 128, 8, 128], SPLIT_PER["wbm"])
        wba_d = SplitAP("wba", [16, 128, 8, 128], SPLIT_PER["wba"])
        wout_d = SplitAP("wout", [4, 128, KC, 512], SPLIT_PER["wout"])
        wfi_d = SplitAP("wfi", [2 * FC, 128, KC, 128], SPLIT_PER["wfi"])
        wfo_d = SplitAP("wfo", [4, 4, 128, 11, 512], SPLIT_PER["wfo"])
        wb_gt = dscr("wb_gt", [32, 128, KC, 128], BF16)
        wb_bm = dscr("wb_bm", [16, 128, 8, 128], BF16)
        wb_ba = dscr("wb_ba", [16, 128, 8, 128], BF16)
        wb_out = dscr("wb_out", [4, 128, KC, 512], BF16)
        wb_fi = dscr("wb_fi", [2 * FC, 128, KC, 128], BF16)
        wb_fo = dscr("wb_fo", [4, 4, 128, 11, 512], BF16)
        for j in range(16):
            convert_plan(flat(wbm_d[j]), flat(wb_bm[j]), "wb_bm%d" % j)
            convert_plan(flat(wba_d[j]), flat(wb_ba[j]), "wb_ba%d" % j)
        for j in range(32):
            convert_plan(flat(win_gt_d[j]), flat(wb_gt[j]), "wb_gt%d" % j)
        for n in range(4):
            convert_plan(flat(wout_d[n]), flat(wb_out[n]), "wb_out%d" % n)
        for j in range(2 * FC):
            convert_plan(flat(wfi_d[j]), flat(wb_fi[j]), "wb_fi%d" % j)
        for n in range(4):
            for g in range(4):
                convert_plan(flat(wfo_d[n, g]), flat(wb_fo[n, g]), "wb_fo%d_%d" % (n, g))

    conv_until(n_upfront)
    if stop == "p0":
        conv_until(len(conv_jobs))
        return finish()

    def lazy_step(n=1):
        for _ in range(n):
            conv_step()

    with ExitStack() as st:
        def sb(name, shape, dt=F32):
            return st.enter_context(nc.sbuf_tensor(name, list(shape), dt))
        xt = [sb("xt%d" % i, [128, D]) for i in range(2)]
        xn = [sb("xn%d" % i, [128, D], BF16) for i in range(2)]
        ss = [sb("ss%d" % i, [128, 1]) for i in range(2)]
        rs = [sb("rs%d" % i, [128, 1]) for i in range(2)]
        hTb = [sb("hTb%d" % i, [128, KC, 512], BF16) for i in range(2)]
        wfm = [sb("wfm%d" % i, [128, KC, 128], BF16) for i in range(3)]
        wtm = [sb("wtm%d" % i, [128, KC, 512], BF16) for i in range(2)]
        wtg = sb("wtg", [128, KC, 16], BF16)
        rc = [sb("rc%d" % i, [128, 512]) for i in range(2)]
        rsn = [sb("rsn%d" % i, [128, 512]) for i in range(2)]
        fmo = [sb("fmo%d" % i, [128, 512], BF16) for i in range(3)]
        qraw = [sb("qraw%d" % i, [128, 512], BF16) for i in range(2)]
        qc = [sb("qc%d" % i, [128, 512]) for i in range(2)]
        qs = [sb("qs%d" % i, [128, 512]) for i in range(2)]
        tmo_b = [sb("tmo_b%d" % i, [128, 512], BF16) for i in range(2)]
        tmo_f = [sb("tmo_f%d" % i, [128, 512]) for i in range(2)]
        gto = [sb("gto%d" % i, [128, 16]) for i in range(2)]
        P.op("sp", lambda e: e.dma_start(out=wtg[:], in_=wb_tg), r=wkeys["wb_tg"], w=["wtg", "once"], dma="once")
        n_wfm, n_wtm, n_fmo, n_ps, n_rope, n_tmo = [0], [0], [0], [0], [0], [0]

        def mm_bank():
            b = 2 + (n_ps[0] % 6)
            n_ps[0] += 1
            return b

        n_lazy_chunks = len(conv_jobs) - n_upfront
        hooks = NB2 * 18
        per_hook = -(-n_lazy_chunks // hooks) if n_lazy_chunks else 0

        pend = []

        def flush(keep):
            while len(pend) > keep:
                pend.pop(0)()

        def norm_block(blk):
            hs = blk % 2
            t0 = blk * 512
            for sub in range(4):
                tt = blk * 4 + sub
                s = tt % 2
                P.op("sp", lambda e, s=s, tt=tt: e.dma_start(out=xt[s][:], in_=x_d.rows(tt * 128, (tt + 1) * 128)),
                     w=["xt%d" % s], dma="xt%d" % s)
                P.op("act", lambda e, s=s: e.activation(out=xn[s][:], in_=xt[s][:], func=AF.Square, accum_out=ss[s][:]),
                     r=["xt%d" % s], w=["xn%d" % s, "ss%d" % s])
                P.op("dve", lambda e, s=s: e.tensor_scalar(out=rs[s][:], in0=ss[s][:], scalar1=1.0 / D, scalar2=EPS,
                                                          op0=ALU.mult, op1=ALU.add),
                     r=["ss%d" % s], w=["rs%d" % s])
                P.op("act", lambda e, s=s: e.sqrt(out=rs[s][:], in_=rs[s][:]), r=["rs%d" % s], w=["rs%d" % s])
                P.op("dve", lambda e, s=s: e.reciprocal(out=rs[s][:], in_=rs[s][:]), r=["rs%d" % s], w=["rs%d" % s])
                P.op("dve", lambda e, s=s: e.tensor_scalar(out=xn[s][:], in0=xt[s][:], scalar1=rs[s][:], scalar2=None,
                                                          op0=ALU.mult),
                     r=["xt%d" % s, "rs%d" % s], w=["xn%d" % s])

                def tr(e, s=s, half=0):
                    ins = None
                    pbv = ps[half].bitcast(BF16)
                    for c in range(8):
                        cc = half * 8 + c
                        ins = e.transpose(out=pbv[:, c * 128:(c + 1) * 128], in_=xn[s][:, cc * 128:(cc + 1) * 128],
                                          identity=ident[:])
                    return ins
                for half in range(2):
                    P.op("pe", lambda e, f=tr, half=half: f(e, half=half), r=["xn%d" % s, "ident"], w=["ps%d" % half])
                    P.op("dve", lambda e, half=half, hs=hs, sub=sub: e.tensor_tensor(
                        out=hTb[hs][:, half * 8:(half + 1) * 8, sub * 128:(sub + 1) * 128],
                        in0=ps[half].bitcast(BF16).rearrange("p (c t) -> p c t", c=8),
                        in1=n1w[:, half * 8:(half + 1) * 8].unsqueeze(2).to_broadcast([128, 8, 128]),
                        op=ALU.mult),
                        r=["ps%d" % half, "n1w"], w=["hTb%d" % hs])
            if blk < NB:
                P.op("sp", lambda e, hs=hs, t0=t0: e.dma_start(out=hT_s[:, :, t0:t0 + 512], in_=hTb[hs][:]),
                     r=["hTb%d" % hs], w=[skey("hT_s", blk)], dma="hTb_st%d" % hs)

        norm_block(0)
        for blk in range(NB2):
            own = blk < NB
            hs = blk % 2
            t0 = blk * 512
            if stop == "norm":
                if blk + 1 < NB2:
                    norm_block(blk + 1)
                continue
            rp = blk % 2
            P.op("sp", lambda e, rp=rp, t0=t0: e.dma_start(out=rc[rp][:], in_=ropec_d[:, t0:t0 + 512]),
                 w=["rc%d" % rp, "rsn%d" % rp], dma="rope%d" % rp)
            P.op("sp", lambda e, rp=rp, t0=t0: e.dma_start(out=rsn[rp][:], in_=ropes_d[:, t0:t0 + 512]),
                 w=["rc%d" % rp, "rsn%d" % rp], dma="rope%d" % rp)
            fm_list = list(range(N_FM)) if own else [4, 5, 6, 7] + list(range(16, 24))
            tm_list = [0, 1, 2, 3, 4, 5] if own else [0, 1, 2, 3]
            wbase = n_wfm[0]
            n_wfm[0] += len(fm_list)
            tbase = n_wtm[0]
            n_wtm[0] += len(tm_list)

            def load_fm(idx, fm_list=fm_list, wbase=wbase):
                j = fm_list[idx]
                ws = (wbase + idx) % 3
                P.op("sp", lambda e, ws=ws, j=j: e.dma_start(out=wfm[ws][:], in_=wb_fm[j]),
                     r=wkeys["wb_fm%d" % j], w=["wfm%d" % ws], dma="wfm%d" % ws)

            def load_tm(idx, tm_list=tm_list, tbase=tbase):
                n = tm_list[idx]
                ws = (tbase + idx) % 2
                P.op("sp", lambda e, ws=ws, n=n: e.dma_start(out=wtm[ws][:], in_=wb_tm[n]),
                     r=wkeys["wb_tm%d" % n], w=["wtm%d" % ws], dma="wtm%d" % ws)

            load_fm(0)
            load_fm(1)
            for idx, j in enumerate(fm_list):
                lazy_step(per_hook)
                if idx + 2 < len(fm_list):
                    load_fm(idx + 2)
                ws = (wbase + idx) % 3
                b = mm_bank()

                def mm(e, ws=ws, b=b, hs=hs):
                    ins = None
                    for kc in range(KC):
                        ins = e.matmul(ps[b], lhsT=wfm[ws][:, kc, :], rhs=hTb[hs][:, kc, :],
                                       start=(kc == 0), stop=(kc == KC - 1))
                    return ins
                P.op("pe", mm, r=["wfm%d" % ws, "hTb%d" % hs], w=["ps%d" % b])
                fo = n_fmo[0] % 3
                n_fmo[0] += 1
                if j < 8:
                    if j < 4:
                        dst, sc, dk = mqT_s[:, j, t0:t0 + 512], 1.0, skey("mqT_s", blk)
                    else:
                        dst, sc, dk = mkT_s[:, j - 4, t0:t0 + 512], MDK ** -0.5, skey("mkT_s", blk)
                    P.op("act", lambda e, fo=fo, b=b, sc=sc: e.mul(out=fmo[fo][:], in_=ps[b], mul=sc),
                         r=["ps%d" % b], w=["fmo%d" % fo])
                else:
                    ri = n_rope[0] % 2
                    n_rope[0] += 1
                    if j < 16:
                        dst, dk = aqT_s[:, j - 8, t0:t0 + 512], skey("aqT_s", blk)
                    else:
                        dst, dk = akT_s[:, j - 16, t0:t0 + 512], skey("akT_s", blk)
                    P.op("act", lambda e, ri=ri, b=b: e.copy(out=qraw[ri][:], in_=ps[b]),
                         r=["ps%d" % b], w=["qraw%d" % ri])
                    P.op("dve", lambda e, ri=ri, b=b, rp=rp: e.tensor_tensor(out=qc[ri][:], in0=ps[b], in1=rc[rp][:],
                                                                            op=ALU.mult),
                         r=["ps%d" % b, "rc%d" % rp], w=["qc%d" % ri])
                    b2 = mm_bank()
                    P.op("pe", lambda e, ri=ri, b2=b2: e.matmul(ps[b2], lhsT=perm[:], rhs=qraw[ri][:], start=True, stop=True),
                         r=["perm", "qraw%d" % ri], w=["ps%d" % b2])
                    P.op("dve", lambda e, ri=ri, b2=b2, rp=rp: e.tensor_tensor(out=qs[ri][:], in0=ps[b2],
                                                                              in1=rsn[rp][:], op=ALU.mult),
                         r=["ps%d" % b2, "rsn%d" % rp], w=["qs%d" % ri])
                    P.op("dve", lambda e, ri=ri, fo=fo: e.tensor_tensor(out=fmo[fo][:], in0=qs[ri][:], in1=qc[ri][:],
                                                                        op=ALU.add),
                         r=["qs%d" % ri, "qc%d" % ri], w=["fmo%d" % fo])
                pend.append(lambda fo=fo, dst=dst, dk=dk: P.op(
                    "sp", lambda e: e.dma_start(out=dst, in_=fmo[fo][:]), r=["fmo%d" % fo], w=[dk], dma="fmo%d" % fo))
                flush(1)
            if stop == "fm":
                if blk + 1 < NB2:
                    norm_block(blk + 1)
                continue
            load_tm(0)
            if blk + 1 < NB2:
                norm_block(blk + 1)
            for idx, n in enumerate(tm_list):
                lazy_step(per_hook)
                if idx + 1 < len(tm_list):
                    load_tm(idx + 1)
                ws = (tbase + idx) % 2
                for sub in range(4):
                    b = mm_bank()

                    def mm2(e, ws=ws, b=b, hs=hs, sub=sub):
                        ins = None
                        for kc in range(KC):
                            ins = e.matmul(ps[b], lhsT=hTb[hs][:, kc, sub * 128:(sub + 1) * 128],
                                           rhs=wtm[ws][:, kc, :], start=(kc == 0), stop=(kc == KC - 1))
                        return ins
                    P.op("pe", mm2, r=["wtm%d" % ws, "hTb%d" % hs], w=["ps%d" % b])
                    to = n_tmo[0] % 2
                    n_tmo[0] += 1
                    r0 = t0 + sub * 128
                    c0 = (n % 2) * 512
                    if n < 4:
                        dstT = mV_s if n < 2 else aV_s
                        dk = skey("mV_s" if n < 2 else "aV_s", blk)
                        P.op("act", lambda e, to=to, b=b: e.copy(out=tmo_b[to][:], in_=ps[b]),
                             r=["ps%d" % b], w=["tmo_b%d" % to])
                        pend.append(lambda to=to, dstT=dstT, r0=r0, c0=c0, dk=dk: P.op(
                            "sp", lambda e: e.dma_start(out=dstT[r0:r0 + 128, c0:c0 + 512], in_=tmo_b[to][:]),
                            r=["tmo_b%d" % to], w=[dk], dma="tmo_b%d" % to))
                    else:
                        P.op("act", lambda e, to=to, b=b: e.activation(out=tmo_f[to][:], in_=ps[b], func=AF.Sigmoid),
                             r=["ps%d" % b], w=["tmo_f%d" % to])
                        pend.append(lambda to=to, r0=r0, c0=c0, blk=blk: P.op(
                            "sp", lambda e: e.dma_start(out=mo_s[r0:r0 + 128, c0:c0 + 512], in_=tmo_f[to][:]),
                            r=["tmo_f%d" % to], w=[skey("mo_s", blk)], dma="tmo_f%d" % to))
                    flush(1)
            for sub in range(4):
                b = mm_bank()
                gi = sub % 2

                def mm3(e, b=b, hs=hs, sub=sub):
                    ins = None
                    for kc in range(KC):
                        ins = e.matmul(ps[b][:, 0:16], lhsT=hTb[hs][:, kc, sub * 128:(sub + 1) * 128], rhs=wtg[:, kc, :],
                                       start=(kc == 0), stop=(kc == KC - 1))
                    return ins
                P.op("pe", mm3, r=["wtg", "hTb%d" % hs], w=["ps%d" % b])
                r0 = t0 + sub * 128
                P.op("dve", lambda e, gi=gi, b=b: e.tensor_tensor(out=gto[gi][:], in0=ps[b][:, 0:16], in1=gbias[:],
                                                                  op=ALU.add),
                     r=["ps%d" % b, "gbias"], w=["gto%d" % gi])
                pend.append(lambda gi=gi, r0=r0, blk=blk: P.op(
                    "sp", lambda e: e.dma_start(out=g_s[r0:r0 + 128, :], in_=gto[gi][:]),
                    r=["gto%d" % gi], w=[skey("g_s", blk)], dma="gto%d" % gi))
                flush(1)
        flush(0)
        conv_until(len(conv_jobs))
    P.barrier()
    cvst.close()
    if stop in ("norm", "fm", "p2"):
        return finish()

    mnw_d = din("mnw", [1, MH * MDV])
    with ExitStack() as st:
        def sb(name, shape, dt=F32):
            return st.enter_context(nc.sbuf_tensor(name, list(shape), dt))
        gall = sb("gall", [128, NC2, 16])
        th = sb("th", [128, NC2, 16])
        gk = []
        for c0 in range(0, NC2, 16):
            n = min(16, NC2 - c0)
            k = "gall%d" % (c0 // 16)
            gk.append(k)
            P.op("sp", lambda e, c0=c0, n=n: e.dma_start(
                out=gall[:, c0:c0 + n, :], in_=g_s[c0 * 128:(c0 + n) * 128, :].rearrange("(c p) j -> p c j", p=128)),
                r=[skey("g_s", b_) for b_ in range(c0 // 4, (c0 + n + 3) // 4)], w=[k], dma="gall")
        P.op("act", lambda e: e.activation(out=th[:], in_=gall[:], func=AF.Tanh, scale=1.0 / CAP), r=gk, w=["th"])
        dirs = {}
        for dn, ncd, co, tri in (("A", NCH, 8, Uf), ("B", NC2, 0, Lf)):
            W = ncd * 4
            IG = sb("IG" + dn, [128, ncd, 4])
            E = sb("E" + dn, [128, ncd, 4])
            LF = sb("LF" + dn, [128, ncd, 4])
            bsb = sb("b" + dn, [128, W])
            r_ = sb("r" + dn, [128, W])
            Rcol = sb("Rcol" + dn, [128, 1])
            Rrow = sb("Rrow" + dn, [1, W])
            bLrow = sb("bLrow" + dn, [1, W])
            Mrow = sb("Mrow" + dn, [1, W])
            larow = sb("larow" + dn, [1, W])
            mrow = sb("mrow" + dn, [1, 4])
            tmp = sb("tmp" + dn, [128, W])
            w_ = sb("w" + dn, [128, ncd, 4])
            cl = sb("cl" + dn, [128, ncd, 4])
            ab = sb("ab" + dn, [128, ncd, 4])
            k = lambda s_, dn=dn: s_ + dn
            P.op("dve", lambda e, IG=IG, ncd=ncd, co=co: e.tensor_scalar(out=IG[:], in0=th[:, 0:ncd, co:co + 4], scalar1=CAP,
                                                                        scalar2=None, op0=ALU.mult), r=["th"], w=[k("IG")])
            P.op("act", lambda e, E=E, ncd=ncd, co=co: e.activation(out=E[:], in_=th[:, 0:ncd, co + 4:co + 8], func=AF.Exp,
                                                                   scale=-CAP), r=["th"], w=[k("E")])
            P.op("act", lambda e, E=E: e.activation(out=E[:], in_=E[:], func=AF.Ln, bias=1.0), r=[k("E")], w=[k("E")])
            P.op("dve", lambda e, E=E, LF=LF: e.tensor_scalar(out=LF[:], in0=E[:], scalar1=-1.0, scalar2=None, op0=ALU.mult),
                 r=[k("E")], w=[k("LF")])
            LFf = LF[:].rearrange("p c h -> p (c h)")
            P.op("pe", lambda e, tri=tri, LFf=LFf, W=W: e.matmul(ps[0][:, 0:W], lhsT=tri, rhs=LFf, start=True, stop=True),
                 r=["cst_f", k("LF")], w=psk(0))
            P.op("pe", lambda e, LFf=LFf, W=W: e.matmul(ps[1][0:1, 0:W], lhsT=ones_f[:, 0:1], rhs=LFf, start=True, stop=True),
                 r=["ones_f", k("LF")], w=psk(1))
            P.op("act", lambda e, bsb=bsb, W=W: e.copy(out=bsb[:], in_=ps[0][:, 0:W]), r=psk(0), w=[k("b")])
            P.op("act", lambda e, bLrow=bLrow, W=W: e.copy(out=bLrow[:], in_=ps[1][0:1, 0:W]), r=psk(1), w=[k("rec")])
            P.op("dve", lambda e, r_=r_, IG=IG, bsb=bsb: e.tensor_tensor(out=r_[:], in0=IG[:].rearrange("p c h -> p (c h)"),
                                                                        in1=bsb[:], op=ALU.subtract),
                 r=[k("IG"), k("b")], w=[k("r")])
            for g0 in range(0, W, 128):
                gw = min(128, W - g0)
                P.op("pe", lambda e, r_=r_, g0=g0, gw=gw: e.transpose(out=ps[2][0:gw, 0:128], in_=r_[:, g0:g0 + gw],
                                                                     identity=identf), r=[k("r"), "cst_f"], w=psk(2))
                P.op("dve", lambda e, Rcol=Rcol, gw=gw: e.tensor_reduce(out=Rcol[0:gw, :], in_=ps[2][0:gw, 0:128], axis=AX.X,
                                                                       op=ALU.max), r=psk(2), w=[k("Rcol")])
                P.op("pe", lambda e, Rcol=Rcol, g0=g0, gw=gw: e.matmul(ps[3][0:1, g0:g0 + gw], lhsT=Rcol[0:gw, 0:1],
                                                                      rhs=cst_f[0:gw, 0:gw], start=True, stop=True),
                     r=[k("Rcol"), "cst_f"], w=psk(3))
            P.op("act", lambda e, Rrow=Rrow, W=W: e.copy(out=Rrow[:], in_=ps[3][0:1, 0:W]), r=psk(3), w=[k("rec")])
            P.op("dve", lambda e, mrow=mrow: e.memset(mrow[:], 0.0), w=[k("rec")])
            order = range(ncd) if dn == "A" else range(ncd - 1, -1, -1)
            for c in order:
                sl = slice(c * 4, c * 4 + 4)
                P.op("dve", lambda e, sl=sl, Mrow=Mrow, mrow=mrow, Rrow=Rrow: e.tensor_tensor(
                    out=Mrow[0:1, sl], in0=mrow[:], in1=Rrow[0:1, sl], op=ALU.max), r=[k("rec")], w=[k("rec")])
                P.op("dve", lambda e, sl=sl, Mrow=Mrow, mrow=mrow, larow=larow: e.tensor_tensor(
                    out=larow[0:1, sl], in0=mrow[:], in1=Mrow[0:1, sl], op=ALU.subtract), r=[k("rec")], w=[k("rec")])
                P.op("dve", lambda e, sl=sl, Mrow=Mrow, mrow=mrow, bLrow=bLrow: e.tensor_tensor(
                    out=mrow[:], in0=bLrow[0:1, sl], in1=Mrow[0:1, sl], op=ALU.add), r=[k("rec")], w=[k("rec")])
            P.op("pe", lambda e, Mrow=Mrow, W=W: e.matmul(ps[4][:, 0:W], lhsT=ones_f[0:1, :], rhs=Mrow[:], start=True, stop=True),
                 r=["ones_f", k("rec")], w=psk(4))
            P.op("pe", lambda e, larow=larow, W=W: e.matmul(ps[5][:, 0:W], lhsT=ones_f[0:1, :], rhs=larow[:], start=True, stop=True),
                 r=["ones_f", k("rec")], w=psk(5))
            P.op("dve", lambda e, tmp=tmp, r_=r_, W=W: e.tensor_tensor(out=tmp[:], in0=r_[:], in1=ps[4][:, 0:W], op=ALU.subtract),
                 r=[k("r")] + psk(4), w=[k("tmp")])
            P.op("act", lambda e, tmp=tmp, w_=w_: e.activation(out=w_[:].rearrange("p c h -> p (c h)"), in_=tmp[:], func=AF.Exp),
                 r=[k("tmp")], w=[k("w")])
            P.op("dve", lambda e, tmp=tmp, bsb=bsb, W=W: e.scalar_tensor_tensor(out=tmp[:], in0=bsb[:], scalar=-1.0,
                                                                               in1=ps[4][:, 0:W], op0=ALU.mult, op1=ALU.subtract),
                 r=[k("b"), k("w")] + psk(4), w=[k("tmp")])
            P.op("act", lambda e, tmp=tmp, cl=cl: e.activation(out=cl[:].rearrange("p c h -> p (c h)"), in_=tmp[:], func=AF.Exp),
                 r=[k("tmp")], w=[k("cl")])
            P.op("act", lambda e, ab=ab, W=W: e.activation(out=ab[:].rearrange("p c h -> p (c h)"), in_=ps[5][:, 0:W], func=AF.Exp),
                 r=psk(5), w=[k("ab")])
            dirs[dn] = dict(w=w_, cl=cl, ab=ab, ncd=ncd, mask=tri)

        kTc = [sb("kTc%d" % i, [128, MH, 128], BF16) for i in range(2)]
        qTc = [sb("qTc%d" % i, [128, MH, 128], BF16) for i in range(2)]
        Vc = [sb("Vc%d" % i, [128, MH, MDV], BF16) for i in range(2)]
        kp = [sb("kp%d" % i, [128, MH, 128], BF16) for i in range(2)]
        Spf = sb("Spf", [128, MH, 128])
        SpT = [sb("SpT%d" % i, [128, MH, 128], BF16) for i in range(2)]
        Cs = sb("Cs", [128, MH, MDV])
        ns = sb("ns", [128, MH])
        Cb = sb("Cb", [128, MH, MDV], BF16)
        nb_ = sb("nb_", [128, MH], BF16)
        dd = sb("dd", [128, MH])
        rec = sb("rec", [128, MH])
        hout = [sb("hout%d" % i, [128, MH, MDV]) for i in range(2)]
        hAc = [sb("hAc%d" % i, [128, MH * MDV]) for i in range(2)]
        moc = [sb("moc%d" % i, [128, MH * MDV]) for i in range(2)]
        mnw_b = sb("mnw_b", [128, MH * MDV])
        ssq = sb("ssq", [128, MH])
        sqj = sb("sqj", [128, MDV], BF16)
        hmg = sb("hmg", [128, MH * MDV], BF16)
        hmTc = [sb("hmTc%d" % i, [128, 8, 128], BF16) for i in range(2)]
        P.op("sp", lambda e: e.dma_start(out=mnw_b[:], in_=mnw_d.partition_broadcast(128)), w=["mnw_b", "once"], dma="once")

        steps = [("A", c, True) for c in range(NCH)] + [("B", c, c < NCH) for c in range(NC2 - 1, -1, -1)]

        def load_step(i):
            d, c, fl = steps[i]
            s = i % 2
            blk = c // 4
            P.op("sp", lambda e, s=s, c=c: e.dma_start(out=kTc[s][:], in_=mkT_s[:, :, c * 128:(c + 1) * 128]),
                 r=[skey("mkT_s", blk)], w=["kTc%d" % s], dma="kTc%d" % s)
            P.op("sp", lambda e, s=s, c=c: e.dma_start(out=Vc[s][:], in_=mV_s[c * 128:(c + 1) * 128, :].rearrange(
                "p (h v) -> p h v", h=MH)), r=[skey("mV_s", blk)], w=["Vc%d" % s], dma="Vc%d" % s)
            if fl:
                P.op("sp", lambda e, s=s, c=c: e.dma_start(out=qTc[s][:], in_=mqT_s[:, :, c * 128:(c + 1) * 128]),
                     r=[skey("mqT_s", blk)], w=["qTc%d" % s], dma="qTc%d" % s)

        def load_late(i):
            d, c, fl = steps[i]
            s = i % 2
            if fl and d == "B":
                P.op("sp", lambda e, s=s, c=c: e.dma_start(out=hAc[s][:], in_=hA_s[c * 128:(c + 1) * 128, :]),
                     r=[skey("hA_s", c)], w=["hAc%d" % s], dma="hAc%d" % s)
                P.op("sp", lambda e, s=s, c=c: e.dma_start(out=moc[s][:], in_=mo_s[c * 128:(c + 1) * 128, :]),
                     r=[skey("mo_s", c // 4)], w=["moc%d" % s], dma="moc%d" % s)

        load_step(0)
        for i, (d, c, fl) in enumerate(steps):
            s = i % 2
            dd_ = dirs[d]
            w_, cl, ab, mask = dd_["w"], dd_["cl"], dd_["ab"], dd_["mask"]
            first = (d == "A" and c == 0) or (d == "B" and c == NC2 - 1)
            if first:
                P.op("dve", lambda e: e.memset(Cs[:], 0.0), w=["Cs"])
                P.op("dve", lambda e: e.memset(ns[:], 0.0), w=["ns"])
            load_late(i)
            if i + 1 < len(steps):
                load_step(i + 1)
            def trk(e, s=s):
                ins = None
                pv = ps[0].bitcast(BF16)
                for h in range(MH):
                    ins = e.transpose(out=pv[:, h * 128:(h + 1) * 128], in_=kTc[s][:, h, :], identity=ident[:])
                return ins
            P.op("pe", trk, r=["kTc%d" % s, "ident"], w=psk(0))
            P.op("dve", lambda e, s=s, c=c, w_=w_: e.tensor_tensor(
                out=kp[s][:], in0=ps[0].bitcast(BF16)[:, 0:512].rearrange("p (h k) -> p h k", h=MH),
                in1=w_[:, c, :].unsqueeze(2).to_broadcast([128, MH, 128]), op=ALU.mult),
                r=psk(0) + ["w" + d], w=["kp%d" % s])
            P.op("dve", lambda e, c=c, ab=ab: e.tensor_tensor(out=Cs[:], in0=Cs[:], in1=ab[:, c, :].unsqueeze(2).to_broadcast(
                [128, MH, MDV]), op=ALU.mult), r=["Cs", "ab" + d], w=["Cs"])
            P.op("dve", lambda e, c=c, ab=ab: e.tensor_tensor(out=ns[:], in0=ns[:], in1=ab[:, c, :], op=ALU.mult),
                 r=["ns", "ab" + d], w=["ns"])
            if fl:
                P.op("act", lambda e: e.copy(out=Cb[:], in_=Cs[:]), r=["Cs"], w=["Cb"])
                P.op("act", lambda e: e.copy(out=nb_[:], in_=ns[:]), r=["ns"], w=["nb_"])

                def mms(e, s=s):
                    ins = None
                    for h in range(MH):
                        ins = e.matmul(ps[1][:, h * 128:(h + 1) * 128], lhsT=kTc[s][:, h, :], rhs=qTc[s][:, h, :],
                                       start=True, stop=True)
                    return ins
                P.op("pe", mms, r=["kTc%d" % s, "qTc%d" % s], w=psk(1))
                P.op("dve", lambda e, c=c, w_=w_: e.tensor_tensor(
                    out=Spf[:], in0=ps[1].rearrange("p (h k) -> p h k", h=MH),
                    in1=w_[:, c, :].unsqueeze(2).to_broadcast([128, MH, 128]), op=ALU.mult),
                    r=psk(1) + ["w" + d], w=["Spf"])
                P.op("dve", lambda e, s=s, mask=mask: e.tensor_tensor(
                    out=SpT[s][:], in0=Spf[:], in1=mask.unsqueeze(1).to_broadcast([128, MH, 128]), op=ALU.mult),
                    r=["Spf", "cst_f"], w=["SpT%d" % s])

                def mmn(e, s=s):
                    ins = None
                    for h in range(MH):
                        o = ps[2 + h // 2][:, (h % 2) * 256:(h % 2) * 256 + 256]
                        e.matmul(o, lhsT=qTc[s][:, h, :], rhs=Cb[:, h, :], start=True, stop=False)
                        e.matmul(o, lhsT=SpT[s][:, h, :], rhs=Vc[s][:, h, :], start=False, stop=True)
                    for h in range(MH):
                        o = ps[6][:, h:h + 1]
                        e.matmul(o, lhsT=qTc[s][:, h, :], rhs=nb_[:, h:h + 1], start=True, stop=False)
                        ins = e.matmul(o, lhsT=SpT[s][:, h, :], rhs=ones_b[:, 0:1], start=False, stop=True)
                    return ins
                P.op("pe", mmn, r=["qTc%d" % s, "Cb", "nb_", "SpT%d" % s, "Vc%d" % s, "ones_b"], w=psk(2, 3, 6))

            def mmc(e, s=s):
                ins = None
                for h in range(MH):
                    o = ps[4 + h // 2][:, (h % 2) * 256:(h % 2) * 256 + 256]
                    e.matmul(o, lhsT=kp[s][:, h, :], rhs=Vc[s][:, h, :], start=True, stop=True)
                for h in range(MH):
                    ins = e.matmul(ps[7][:, h:h + 1], lhsT=kp[s][:, h, :], rhs=ones_b[:, 0:1], start=True, stop=True)
                return ins
            P.op("pe", mmc, r=["kp%d" % s, "Vc%d" % s, "ones_b"], w=psk(4, 5, 7))
            if fl:
                P.op("act", lambda e: e.activation(out=dd[:], in_=ps[6][:, 0:MH], func=AF.Abs), r=psk(6), w=["dd"])
                P.op("dve", lambda e, c=c, cl=cl: e.tensor_tensor(out=dd[:], in0=dd[:], in1=cl[:, c, :], op=ALU.max),
                     r=["dd", "cl" + d], w=["dd"])
                P.op("dve", lambda e: e.reciprocal(out=rec[:], in_=dd[:]), r=["dd"], w=["rec"])
                for hh in range(2):
                    P.op("dve", lambda e, s=s, hh=hh: e.tensor_tensor(
                        out=hout[s][:, hh * 2:hh * 2 + 2, :], in0=ps[2 + hh].rearrange("p (h v) -> p h v", h=2),
                        in1=rec[:, hh * 2:hh * 2 + 2].unsqueeze(2).to_broadcast([128, 2, MDV]), op=ALU.mult),
                        r=psk(2 + hh) + ["rec"], w=["hout%d" % s])
            for hh in range(2):
                P.op("dve", lambda e, hh=hh: e.tensor_tensor(
                    out=Cs[:, hh * 2:hh * 2 + 2, :], in0=Cs[:, hh * 2:hh * 2 + 2, :],
                    in1=ps[4 + hh].rearrange("p (h v) -> p h v", h=2), op=ALU.add), r=psk(4 + hh) + ["Cs"], w=["Cs"])
            P.op("dve", lambda e: e.tensor_tensor(out=ns[:], in0=ns[:], in1=ps[7][:, 0:MH], op=ALU.add), r=psk(7) + ["ns"], w=["ns"])
            if fl and d == "A":
                P.op("sp", lambda e, s=s, c=c: e.dma_start(out=hA_s[c * 128:(c + 1) * 128, :],
                                                          in_=hout[s][:].rearrange("p h v -> p (h v)")),
                     r=["hout%d" % s], w=[skey("hA_s", c)], dma="hout%d" % s)
            if fl and d == "B":
                hf = hout[s][:].rearrange("p h v -> p (h v)")
                P.op("dve", lambda e, s=s, hf=hf: e.tensor_tensor(out=hf, in0=hf, in1=hAc[s][:], op=ALU.add),
                     r=["hout%d" % s, "hAc%d" % s], w=["hout%d" % s])
                for h in range(MH):
                    P.op("act", lambda e, s=s, h=h: e.activation(out=sqj[:], in_=hout[s][:, h, :], func=AF.Square,
                                                                accum_out=ssq[:, h:h + 1]),
                         r=["hout%d" % s], w=["sqj", "ssq"])
                P.op("dve", lambda e: e.tensor_scalar(out=ssq[:], in0=ssq[:], scalar1=1.0 / MDV, scalar2=EPS, op0=ALU.mult,
                                                      op1=ALU.add), r=["ssq"], w=["ssq"])
                P.op("act", lambda e: e.sqrt(out=ssq[:], in_=ssq[:]), r=["ssq"], w=["ssq"])
                P.op("dve", lambda e: e.reciprocal(out=ssq[:], in_=ssq[:]), r=["ssq"], w=["ssq"])
                for h in range(MH):
                    P.op("dve", lambda e, s=s, h=h: e.scalar_tensor_tensor(
                        out=hout[s][:, h, :], in0=hout[s][:, h, :], scalar=ssq[:, h:h + 1],
                        in1=mnw_b[:, h * MDV:(h + 1) * MDV], op0=ALU.mult, op1=ALU.mult),
                        r=["hout%d" % s, "ssq", "mnw_b"], w=["hout%d" % s])
                P.op("dve", lambda e, s=s, hf=hf: e.tensor_tensor(out=hmg[:], in0=hf, in1=moc[s][:], op=ALU.mult),
                     r=["hout%d" % s, "moc%d" % s], w=["hmg"])

                def trh(e):
                    ins = None
                    pv = ps[1].bitcast(BF16)
                    for cc in range(8):
                        ins = e.transpose(out=pv[:, cc * 128:(cc + 1) * 128], in_=hmg[:, cc * 128:(cc + 1) * 128],
                                          identity=ident[:])
                    return ins
                P.op("pe", trh, r=["hmg", "ident"], w=psk(1))
                P.op("act", lambda e, s=s: e.copy(out=hmTc[s][:], in_=ps[1].bitcast(BF16).rearrange("p (c t) -> p c t", c=8)),
                     r=psk(1), w=["hmTc%d" % s])
                P.op("sp", lambda e, s=s, c=c: e.dma_start(out=hmT_s[:, :, c * 128:(c + 1) * 128], in_=hmTc[s][:]),
                     r=["hmTc%d" % s], w=[skey("hmT_s", c // 4)], dma="hmTc%d" % s)
    P.barrier()
    if stop == "p3":
        return finish()

    anw_d = din("anw", [128, 1])
    lam_d = din("lamv", [1, 256])
    nlam = gsb("nlam", [128, 1])
    with ExitStack() as st:
        def sb(name, shape, dt=F32):
            return st.enter_context(nc.sbuf_tensor(name, list(shape), dt))
        lamr = sb("lamr", [1, 256])
        lprod = sb("lprod", [1, 128])
        ldot = sb("ldot", [1, 2])
        anws = sb("anws", [128, 1])
        P.op("sp", lambda e: e.dma_start(out=lamr[:], in_=lam_d), w=["lamr", "once"], dma="once")
        P.op("sp", lambda e: e.dma_start(out=anws[:], in_=anw_d), w=["anws", "once"], dma="once")
        P.op("dve", lambda e: e.tensor_tensor(out=lprod[:].rearrange("p (a k) -> p a k", a=2),
                                              in0=lamr[:].rearrange("p (a b k) -> p a b k", a=2, b=2)[:, :, 0, :],
                                              in1=lamr[:].rearrange("p (a b k) -> p a b k", a=2, b=2)[:, :, 1, :],
                                              op=ALU.mult), r=["lamr"], w=["lprod"])
        P.op("dve", lambda e: e.tensor_reduce(out=ldot[:], in_=lprod[:].rearrange("p (a k) -> p a k", a=2), axis=AX.X,
                                              op=ALU.add), r=["lprod"], w=["ldot"])
        P.op("act", lambda e: e.activation(out=ldot[:], in_=ldot[:], func=AF.Exp), r=["ldot"], w=["ldot"])
        P.op("dve", lambda e: e.tensor_tensor(out=lprod[0:1, 0:1], in0=ldot[0:1, 1:2], in1=ldot[0:1, 0:1], op=ALU.subtract),
             r=["ldot", "lprod"], w=["lprod"])
        P.op("dve", lambda e: e.tensor_scalar(out=lprod[0:1, 0:1], in0=lprod[0:1, 0:1], scalar1=-LAM_INIT, scalar2=None,
                                              op0=ALU.add), r=["lprod"], w=["lprod"])
        P.op("pe", lambda e: e.matmul(ps[0][:, 0:1], lhsT=ones_f[0:1, :], rhs=lprod[0:1, 0:1], start=True, stop=True),
             r=["ones_f", "lprod"], w=psk(0))
        P.op("act", lambda e: e.copy(out=nlam[:], in_=ps[0][:, 0:1]), r=psk(0), w=["nlam"])
        P.op("dve", lambda e: e.tensor_scalar(out=anws[:], in0=anws[:], scalar1=1.0 - LAM_INIT, scalar2=None, op0=ALU.mult),
             r=["anws"], w=["anws"])

        NKT = T2 // 128
        akT = [sb("akT%d" % i, [128, T2], BF16) for i in range(2)]
        aVh = [sb("aVh%d" % i, [128, NKT, ADV], BF16) for i in range(2)]
        aq = [sb("aq%d" % i, [128, 512], BF16) for i in range(2)]
        Pt = [sb("Pt%d" % i, [128, 1024], BF16) for i in range(2)]
        zacc = [sb("zacc%d" % i, [128, 1024]) for i in range(2)]
        r0_ = sb("r0_", [128, 512])
        r1_ = sb("r1_", [128, 512])
        t0_ = sb("t0_", [128, 512])
        t1_ = sb("t1_", [128, 512])
        sq_ = sb("sq_", [128, 512])
        rstd_ = sb("rstd_", [128, 512])
        hao = [sb("hao%d" % i, [128, 512], BF16) for i in range(2)]
        SC = ADH ** -0.5
        nqb = 0
        for h in range(AH):
            hs = h % 2
            P.op("sp", lambda e, hs=hs, h=h: e.dma_start(out=akT[hs][:], in_=akT_s[:, h, :]),
                 r=[skey("akT_s", b_) for b_ in range(NB2)], w=["akT%d" % hs], dma="akT%d" % hs)
            for k0 in range(0, NKT, 16):
                kn = min(16, NKT - k0)
                P.op("sp", lambda e, hs=hs, h=h, k0=k0, kn=kn: e.dma_start(
                    out=aVh[hs][:, k0:k0 + kn, :],
                    in_=aV_s[k0 * 128:(k0 + kn) * 128, h * ADV:(h + 1) * ADV].rearrange("(k p) v -> p k v", p=128)),
                    r=[skey("aV_s", b_) for b_ in range(NB2)], w=["aVh%d_%d" % (hs, k0)], dma="aVh%d" % hs)
            avk = ["aVh%d_%d" % (hs, k0) for k0 in range(0, NKT, 16)]
            for qb in range(NB):
                qs_ = nqb % 2
                nqb += 1
                P.op("sp", lambda e, qs_=qs_, h=h, qb=qb: e.dma_start(out=aq[qs_][:], in_=aqT_s[:, h, qb * 512:(qb + 1) * 512]),
                     r=[skey("aqT_s", qb)], w=["aq%d" % qs_], dma="aq%d" % qs_)

                def qk(kt, hs=hs, qs_=qs_):
                    pi = kt % 2

                    def f(e):
                        e.matmul(pp[pi][:, 0:512], lhsT=akT[hs][0:64, kt * 128:(kt + 1) * 128], rhs=aq[qs_][0:64, :],
                                 start=True, stop=True)
                        return e.matmul(pp[pi][:, 512:1024], lhsT=akT[hs][64:128, kt * 128:(kt + 1) * 128],
                                        rhs=aq[qs_][64:128, :], start=True, stop=True)
                    P.op("pe", f, r=["akT%d" % hs, "aq%d" % qs_], w=psk(2 * pi, 2 * pi + 1))

                def pv(kt, hs=hs, za=qs_):
                    pi = kt % 2
                    P.op("act", lambda e: e.activation(out=Pt[pi][:], in_=pp[pi][:], func=AF.Exp, scale=SC),
                         r=psk(2 * pi, 2 * pi + 1), w=["Pt%d" % pi])

                    def f(e):
                        st_, sp_ = (kt == 0), (kt == NKT - 1)
                        e.matmul(ps[4], lhsT=aVh[hs][:, kt, :], rhs=Pt[pi][:, 0:512], start=st_, stop=sp_)
                        return e.matmul(ps[5], lhsT=aVh[hs][:, kt, :], rhs=Pt[pi][:, 512:1024], start=st_, stop=sp_)
                    P.op("pe", f, r=avk + ["Pt%d" % pi], w=psk(4, 5))
                    if kt == 0:
                        P.op("dve", lambda e: e.tensor_copy(out=zacc[za][:], in_=Pt[pi][:]), r=["Pt%d" % pi], w=["zacc%d" % za])
                    else:
                        P.op("dve", lambda e: e.tensor_tensor(out=zacc[za][:], in0=zacc[za][:], in1=Pt[pi][:], op=ALU.add),
                             r=["Pt%d" % pi, "zacc%d" % za], w=["zacc%d" % za])
                qk(0)
                for kt in range(NKT):
                    if kt + 1 < NKT:
                        qk(kt + 1)
                    pv(kt)
                ho = nqb % 2
                P.op("pe", lambda e, za=qs_: e.matmul(ps[6], lhsT=ones_f[:], rhs=zacc[za][:, 0:512], start=True, stop=True),
                     r=["ones_f", "zacc%d" % qs_], w=psk(6))
                P.op("pe", lambda e, za=qs_: e.matmul(ps[7], lhsT=ones_f[:], rhs=zacc[za][:, 512:1024], start=True, stop=True),
                     r=["ones_f", "zacc%d" % qs_], w=psk(7))
                P.op("dve", lambda e: e.reciprocal(out=r0_[:], in_=ps[6]), r=psk(6), w=["r0_"])
                P.op("dve", lambda e: e.reciprocal(out=r1_[:], in_=ps[7]), r=psk(7), w=["r1_"])
                P.op("dve", lambda e: e.tensor_tensor(out=t0_[:], in0=ps[4], in1=r0_[:], op=ALU.mult), r=psk(4) + ["r0_"], w=["t0_"])
                P.op("dve", lambda e: e.tensor_tensor(out=t1_[:], in0=ps[5], in1=r1_[:], op=ALU.mult), r=psk(5) + ["r1_"], w=["t1_"])
                P.op("dve", lambda e: e.scalar_tensor_tensor(out=t0_[:], in0=t1_[:], scalar=nlam[:, 0:1], in1=t0_[:],
                                                             op0=ALU.mult, op1=ALU.add), r=["t0_", "t1_", "nlam"], w=["t0_"])
                P.op("act", lambda e: e.activation(out=sq_[:], in_=t0_[:], func=AF.Square), r=["t0_"], w=["sq_"])
                P.op("pe", lambda e: e.matmul(ps[6], lhsT=ones_f[:], rhs=sq_[:], start=True, stop=True), r=["ones_f", "sq_"], w=psk(6))
                P.op("dve", lambda e: e.tensor_scalar(out=rstd_[:], in0=ps[6], scalar1=1.0 / ADV, scalar2=EPS, op0=ALU.mult,
                                                      op1=ALU.add), r=psk(6), w=["rstd_"])
                P.op("act", lambda e: e.sqrt(out=rstd_[:], in_=rstd_[:]), r=["rstd_"], w=["rstd_"])
                P.op("dve", lambda e: e.reciprocal(out=rstd_[:], in_=rstd_[:]), r=["rstd_"], w=["rstd_"])
                P.op("dve", lambda e, ho=ho: e.scalar_tensor_tensor(out=hao[ho][:], in0=t0_[:], scalar=anws[:, 0:1], in1=rstd_[:],
                                                                    op0=ALU.mult, op1=ALU.mult),
                     r=["t0_", "anws", "rstd_"], w=["hao%d" % ho])
                P.op("sp", lambda e, ho=ho, h=h, qb=qb: e.dma_start(out=haT_s[:, h, qb * 512:(qb + 1) * 512], in_=hao[ho][:]),
                     r=["hao%d" % ho], w=[skey("haT_s", qb) + ".%d" % h], dma="hao%d" % ho)
                final_keys.append(skey("haT_s", qb) + ".%d" % h)
    P.barrier()
    if stop == "p4":
        return finish()

    n2w_d = din("n2w", [128, KC])
    fnw_d = din("fnw", [1, D])
    bgt_d = din("bgt", [128, 32])
    with ExitStack() as st:
        def sb(name, shape, dt=F32):
            return st.enter_context(nc.sbuf_tensor(name, list(shape), dt))
        UU = sb("UU", [128, 16384], BF16)
        hT_b = UU[:, 0:8192].rearrange("p (k t) -> p k t", k=KC)
        hmT_b = UU[:, 8192:12288].rearrange("p (k t) -> p k t", k=8)
        haT_b = UU[:, 12288:16384].rearrange("p (k t) -> p k t", k=8)
        actT = [UU[:, g * 5632:(g + 1) * 5632].rearrange("p (k t) -> p k t", k=11) for g in range(2)]
        UK = ["UU0", "UU1", "UU2"]
        wbm_t = [sb("wbm_t%d" % i, [128, 8, 128], BF16) for i in range(2)]
        wba_t = [sb("wba_t%d" % i, [128, 8, 128], BF16) for i in range(2)]
        wgm_t = [sb("wgm_t%d" % i, [128, KC, 128], BF16) for i in range(2)]
        wga_t = [sb("wga_t%d" % i, [128, KC, 128], BF16) for i in range(2)]
        sgm = sb("sgm", [128, 512])
        sga = sb("sga", [128, 512])
        mT = sb("mT", [128, KC, 512], BF16)
        wo_t = [sb("wo_t%d" % i, [128, 4, 512], BF16) for i in range(4)]
        xacc = sb("xacc", [128, 4, D])
        xn2 = sb("xn2", [128, D], BF16)
        h2T = sb("h2T", [128, KC, 512], BF16)
        wfg_t = [sb("wfg_t%d" % i, [128, KC, 128], BF16) for i in range(2)]
        wfu_t = [sb("wfu_t%d" % i, [128, KC, 128], BF16) for i in range(2)]
        sl_ = [sb("sl_%d" % i, [128, 512]) for i in range(2)]
        wfo_t = [sb("wfo_t%d" % i, [128, 11, 512], BF16) for i in range(2)]
        fnw_b = sb("fnw_b", [128, D])
        n2w = sb("n2w_sb", [128, KC])
        bgt = sb("bgt_sb", [128, 32])
        ss5 = sb("ss5", [128, 1])
        rs5 = sb("rs5", [128, 1])
        P.op("sp", lambda e: e.dma_start(out=fnw_b[:], in_=fnw_d.partition_broadcast(128)), w=["fnw_b", "once"], dma="once")
        P.op("sp", lambda e: e.dma_start(out=n2w[:], in_=n2w_d), w=["n2w", "once"], dma="once")
        P.op("sp", lambda e: e.dma_start(out=bgt[:], in_=bgt_d), w=["bgt", "once"], dma="once")
        nw5, nwo, nwf, nwfo, nsl = [0], [0], [0], [0], [0]

        def rms_rstd(src_ap, junk_ap, srckeys, junkkeys):
            P.op("act", lambda e: e.activation(out=junk_ap, in_=src_ap, func=AF.Square, accum_out=ss5[:]),
                 r=srckeys, w=junkkeys + ["ss5"])
            P.op("dve", lambda e: e.tensor_scalar(out=rs5[:], in0=ss5[:], scalar1=1.0 / D, scalar2=EPS, op0=ALU.mult,
                                                  op1=ALU.add), r=["ss5"], w=["rs5"])
            P.op("act", lambda e: e.sqrt(out=rs5[:], in_=rs5[:]), r=["rs5"], w=["rs5"])
            P.op("dve", lambda e: e.reciprocal(out=rs5[:], in_=rs5[:]), r=["rs5"], w=["rs5"])

        for blk in range(NB):
            t0 = blk * 512
            P.op("sp", lambda e, t0=t0: e.dma_start(out=hT_b, in_=hT_s[:, :, t0:t0 + 512]),
                 r=[skey("hT_s", blk)], w=UK[0:2], dma="ld_hT")
            P.op("sp", lambda e, t0=t0: e.dma_start(out=hmT_b, in_=hmT_s[:, :, t0:t0 + 512]),
                 r=[skey("hmT_s", blk)], w=UK[1:3], dma="ld_hm")
            P.op("sp", lambda e, t0=t0: e.dma_start(out=haT_b, in_=haT_s[:, :, t0:t0 + 512]),
                 r=[skey("haT_s", blk) + ".%d" % h for h in range(AH)], w=UK[2:3], dma="ld_ha")
            for sub in range(4):
                P.op("sp", lambda e, t0=t0, sub=sub: e.dma_start(out=xacc[:, sub, :], in_=x_d.rows(t0 + sub * 128, t0 + (sub + 1) * 128)),
                     w=["xacc%d" % sub], dma="xacc%d" % sub)
            for j in range(16):
                ws = nw5[0] % 2
                nw5[0] += 1
                P.op("sp", lambda e, ws=ws, j=j: e.dma_start(out=wbm_t[ws][:], in_=wb_bm[j]),
                     r=wkeys["wb_bm%d" % j], w=["wbm_t%d" % ws], dma="w5_%d" % ws)
                P.op("sp", lambda e, ws=ws, j=j: e.dma_start(out=wba_t[ws][:], in_=wb_ba[j]),
                     r=wkeys["wb_ba%d" % j], w=["wba_t%d" % ws], dma="w5_%d" % ws)
                P.op("sp", lambda e, ws=ws, j=j: e.dma_start(out=wgm_t[ws][:], in_=wb_gt[j]),
                     r=wkeys["wb_gt%d" % j], w=["wgm_t%d" % ws], dma="w5_%d" % ws)
                P.op("sp", lambda e, ws=ws, j=j: e.dma_start(out=wga_t[ws][:], in_=wb_gt[16 + j]),
                     r=wkeys["wb_gt%d" % (16 + j)], w=["wga_t%d" % ws], dma="w5_%d" % ws)
                pb = 4 * (j % 2)

                def mm5(e, ws=ws, pb=pb):
                    ins = None
                    for kc in range(8):
                        e.matmul(ps[pb], lhsT=wbm_t[ws][:, kc, :], rhs=hmT_b[:, kc, :], start=(kc == 0), stop=(kc == 7))
                    for kc in range(8):
                        e.matmul(ps[pb + 1], lhsT=wba_t[ws][:, kc, :], rhs=haT_b[:, kc, :], start=(kc == 0), stop=(kc == 7))
                    for kc in range(KC):
                        e.matmul(ps[pb + 2], lhsT=wgm_t[ws][:, kc, :], rhs=hT_b[:, kc, :], start=(kc == 0), stop=(kc == KC - 1))
                    for kc in range(KC):
                        ins = e.matmul(ps[pb + 3], lhsT=wga_t[ws][:, kc, :], rhs=hT_b[:, kc, :], start=(kc == 0),
                                       stop=(kc == KC - 1))
                    return ins
                P.op("pe", mm5, r=["wbm_t%d" % ws, "wba_t%d" % ws, "wgm_t%d" % ws, "wga_t%d" % ws] + UK,
                     w=psk(pb, pb + 1, pb + 2, pb + 3))
                P.op("act", lambda e, pb=pb, j=j: e.activation(out=sgm[:], in_=ps[pb + 2], func=AF.Sigmoid, bias=bgt[:, j:j + 1]),
                     r=psk(pb + 2) + ["bgt"], w=["sgm"])
                P.op("act", lambda e, pb=pb, j=j: e.activation(out=sga[:], in_=ps[pb + 3], func=AF.Sigmoid,
                                                              bias=bgt[:, 16 + j:17 + j]), r=psk(pb + 3) + ["bgt"], w=["sga"])
                P.op("dve", lambda e, pb=pb: e.tensor_tensor(out=sgm[:], in0=sgm[:], in1=ps[pb], op=ALU.mult),
                     r=psk(pb) + ["sgm"], w=["sgm"])
                P.op("dve", lambda e, pb=pb: e.tensor_tensor(out=sga[:], in0=sga[:], in1=ps[pb + 1], op=ALU.mult),
                     r=psk(pb + 1) + ["sga"], w=["sga"])
                P.op("dve", lambda e, j=j: e.tensor_tensor(out=mT[:, j, :], in0=sgm[:], in1=sga[:], op=ALU.add),
                     r=["sgm", "sga"], w=["mT"])
            for nb in range(4):
                pb = 4 * (nb % 2)
                for kg in range(4):
                    wsl = nwo[0] % 4
                    nwo[0] += 1
                    P.op("sp", lambda e, wsl=wsl, nb=nb, kg=kg: e.dma_start(out=wo_t[wsl][:], in_=wb_out[nb][:, kg * 4:(kg + 1) * 4, :]),
                         r=wkeys["wb_out%d" % nb], w=["wo_t%d" % wsl], dma="wo_t%d" % wsl)

                    def mmo(e, wsl=wsl, kg=kg, pb=pb):
                        ins = None
                        for sub in range(4):
                            for kl in range(4):
                                kc = kg * 4 + kl
                                ins = e.matmul(ps[pb + sub], lhsT=mT[:, kc, sub * 128:(sub + 1) * 128], rhs=wo_t[wsl][:, kl, :],
                                               start=(kc == 0), stop=(kc == KC - 1))
                        return ins
                    P.op("pe", mmo, r=["mT", "wo_t%d" % wsl], w=psk(pb, pb + 1, pb + 2, pb + 3))
                for sub in range(4):
                    P.op("dve", lambda e, sub=sub, nb=nb, pb=pb: e.tensor_tensor(
                        out=xacc[:, sub, nb * 512:(nb + 1) * 512], in0=xacc[:, sub, nb * 512:(nb + 1) * 512],
                        in1=ps[pb + sub], op=ALU.add), r=psk(pb + sub) + ["xacc%d" % sub], w=["xacc%d" % sub])
            for sub in range(4):
                rms_rstd(xacc[:, sub, :], xn2[:], ["xacc%d" % sub], ["xn2"])
                P.op("dve", lambda e, sub=sub: e.tensor_scalar(out=xn2[:], in0=xacc[:, sub, :], scalar1=rs5[:], scalar2=None,
                                                              op0=ALU.mult), r=["xacc%d" % sub, "rs5"], w=["xn2"])
                for half in range(2):
                    def tr2(e, half=half):
                        ins = None
                        pbv = ps[half].bitcast(BF16)
                        for c in range(8):
                            cc = half * 8 + c
                            ins = e.transpose(out=pbv[:, c * 128:(c + 1) * 128], in_=xn2[:, cc * 128:(cc + 1) * 128],
                                              identity=ident[:])
                        return ins
                    P.op("pe", tr2, r=["xn2", "ident"], w=psk(half))
                    P.op("dve", lambda e, half=half, sub=sub: e.tensor_tensor(
                        out=h2T[:, half * 8:(half + 1) * 8, sub * 128:(sub + 1) * 128],
                        in0=ps[half].bitcast(BF16).rearrange("p (c t) -> p c t", c=8),
                        in1=n2w[:, half * 8:(half + 1) * 8].unsqueeze(2).to_broadcast([128, 8, 128]), op=ALU.mult),
                        r=psk(half) + ["n2w"], w=["h2T"])
            for g in range(4):
                ga = g % 2
                ak = UK[0:1] if ga == 0 else UK[1:2]
                for jj in range(11):
                    j = g * 11 + jj
                    ws = nwf[0] % 2
                    nwf[0] += 1
                    P.op("sp", lambda e, ws=ws, j=j: e.dma_start(out=wfg_t[ws][:], in_=wb_fi[j]),
                         r=wkeys["wb_fi%d" % j], w=["wfg_t%d" % ws], dma="wf_%d" % ws)
                    P.op("sp", lambda e, ws=ws, j=j: e.dma_start(out=wfu_t[ws][:], in_=wb_fi[FC + j]),
                         r=wkeys["wb_fi%d" % (FC + j)], w=["wfu_t%d" % ws], dma="wf_%d" % ws)
                    pb = 2 * (jj % 2)

                    def mmf(e, ws=ws, pb=pb):
                        ins = None
                        for kc in range(KC):
                            e.matmul(ps[pb], lhsT=wfg_t[ws][:, kc, :], rhs=h2T[:, kc, :], start=(kc == 0), stop=(kc == KC - 1))
                        for kc in range(KC):
                            ins = e.matmul(ps[pb + 1], lhsT=wfu_t[ws][:, kc, :], rhs=h2T[:, kc, :], start=(kc == 0),
                                           stop=(kc == KC - 1))
                        return ins
                    P.op("pe", mmf, r=["wfg_t%d" % ws, "wfu_t%d" % ws, "h2T"], w=psk(pb, pb + 1))
                    si = nsl[0] % 2
                    nsl[0] += 1
                    P.op("act", lambda e, si=si, pb=pb: e.activation(out=sl_[si][:], in_=ps[pb], func=AF.Silu),
                         r=psk(pb), w=["sl_%d" % si])
                    P.op("dve", lambda e, si=si, pb=pb, ga=ga, jj=jj: e.tensor_tensor(out=actT[ga][:, jj, :], in0=sl_[si][:],
                                                                                     in1=ps[pb + 1], op=ALU.mult),
                         r=psk(pb + 1) + ["sl_%d" % si], w=ak)
                for nb in range(4):
                    ws = nwfo[0] % 2
                    nwfo[0] += 1
                    P.op("sp", lambda e, ws=ws, nb=nb, g=g: e.dma_start(out=wfo_t[ws][:], in_=wb_fo[nb, g]),
                         r=wkeys["wb_fo%d_%d" % (nb, g)], w=["wfo_t%d" % ws], dma="wfo_t%d" % ws)
                    for sub in range(4):
                        def mmo2(e, ws=ws, sub=sub, ga=ga):
                            ins = None
                            for kk in range(11):
                                ins = e.matmul(ps[4 + sub], lhsT=actT[ga][:, kk, sub * 128:(sub + 1) * 128], rhs=wfo_t[ws][:, kk, :],
                                               start=(kk == 0), stop=(kk == 10))
                            return ins
                        P.op("pe", mmo2, r=ak + ["wfo_t%d" % ws], w=psk(4 + sub))
                        P.op("dve", lambda e, sub=sub, nb=nb: e.tensor_tensor(
                            out=xacc[:, sub, nb * 512:(nb + 1) * 512], in0=xacc[:, sub, nb * 512:(nb + 1) * 512],
                            in1=ps[4 + sub], op=ALU.add), r=psk(4 + sub) + ["xacc%d" % sub], w=["xacc%d" % sub])
            for sub in range(4):
                rms_rstd(xacc[:, sub, :], xn2[:], ["xacc%d" % sub], ["xn2"])
                P.op("dve", lambda e, sub=sub: e.scalar_tensor_tensor(out=xacc[:, sub, :], in0=xacc[:, sub, :], scalar=rs5[:, 0:1],
                                                                     in1=fnw_b[:], op0=ALU.mult, op1=ALU.mult),
                     r=["xacc%d" % sub, "rs5", "fnw_b"], w=["xacc%d" % sub])
                ok = "out:%d:%d" % (blk, sub)
                final_keys.append(ok)
                P.op("sp", lambda e, sub=sub, t0=t0: e.dma_start(out=out_d[t0 + sub * 128:t0 + (sub + 1) * 128, :], in_=xacc[:, sub, :]),
                     r=["xacc%d" % sub], w=[ok], dma="xacc_st%d" % sub)
    return finish()


def _tiles_stat(W, cols, kc):
    out = np.empty((len(cols), 128, kc, 128), np.float32)
    for i, c0 in enumerate(cols):
        out[i] = W[:, c0:c0 + 128].reshape(kc, 128, 128).transpose(1, 0, 2)
    return out


def _tiles_mov(W, kc):
    return np.ascontiguousarray(W.reshape(kc, 128, W.shape[1]).transpose(1, 0, 2))


def prep_shared(inp):
    w_in = np.asarray(inp["w_in"][0], np.float32)
    sh = {}
    fm_cols = [O_MQ + h * 128 for h in range(4)] + [O_MK + h * 128 for h in range(4)] + \
              [O_AQ + h * 128 for h in range(8)] + [O_AK + h * 128 for h in range(8)]
    sh["win_fm"] = _tiles_stat(w_in, fm_cols, KC)
    sh["win_gt"] = _tiles_stat(w_in, [O_GT + j * 128 for j in range(32)], KC)
    sh["wbm"] = _tiles_stat(np.asarray(inp["w_branch_m"][0], np.float32), [j * 128 for j in range(16)], 8)
    sh["wba"] = _tiles_stat(np.asarray(inp["w_branch_a"][0], np.float32), [j * 128 for j in range(16)], 8)
    wout = np.asarray(inp["w_out"][0], np.float32)
    sh["wout"] = np.stack([_tiles_mov(wout[:, n * 512:(n + 1) * 512], KC) for n in range(4)])
    wfi = np.asarray(inp["w_ffn_in"][0], np.float32)
    sh["wfi"] = _tiles_stat(wfi, [j * 128 for j in range(2 * FC)], KC)
    wfo = np.asarray(inp["w_ffn_out"][0], np.float32)
    t = wfo.reshape(4, 11, 128, 4, 512)
    sh["wfo"] = np.ascontiguousarray(t.transpose(3, 0, 2, 1, 4))
    sh["n1w"] = np.ascontiguousarray(np.asarray(inp["norm1_w"][0], np.float32).reshape(KC, 128).T)
    sh["n2w"] = np.ascontiguousarray(np.asarray(inp["norm2_w"][0], np.float32).reshape(KC, 128).T)
    sh["fnw"] = np.asarray(inp["final_norm_w"], np.float32).reshape(1, D)
    sh["mnw"] = np.asarray(inp["mlstm_norm_w"][0], np.float32).reshape(1, MH * MDV)
    sh["anw"] = np.asarray(inp["attn_norm_w"][0], np.float32).reshape(128, 1)
    sh["bgt"] = np.ascontiguousarray(np.asarray(inp["b_branch_gate"][0], np.float32).reshape(32, 128).T)
    sh["lamv"] = np.concatenate([np.asarray(inp[k][0], np.float32) for k in ("lam_q1", "lam_k1", "lam_q2", "lam_k2")]).reshape(1, 256)
    ident = np.eye(128, dtype=np.float32)
    perm = np.zeros((128, 128), np.float32)
    for m in range(128):
        pm = m + 32 if (m % 64) < 32 else m - 32
        perm[pm, m] = 1.0
    U = np.triu(np.ones((128, 128), np.float32))
    L = np.tril(np.ones((128, 128), np.float32))
    sh["cst"] = np.concatenate([ident, perm, U, L], axis=1)
    sh["_w_in"] = w_in
    return sh


def prep_core(inp, sh, b, half, S):
    x = np.asarray(inp["x"][b], np.float32)
    pos = np.arange(S, dtype=np.float32)
    if half == 1:
        x = x[::-1]
        pos = pos[::-1]
    dA, dB = (0, 1) if half == 0 else (1, 0)
    w_in = sh["_w_in"]
    gcols = [O_MG + dB * 8 + k * 4 + h for k in range(2) for h in range(4)] + \
            [O_MG + dA * 8 + k * 4 + h for k in range(2) for h in range(4)]
    wtm = np.concatenate([w_in[:, O_MV:O_MV + 1024], w_in[:, O_AV:O_AV + 1024], w_in[:, O_MO:O_MO + 1024],
                          w_in[:, gcols]], axis=1)
    bi = np.asarray(inp["b_igate"][0], np.float32)
    bf = np.asarray(inp["b_fgate"][0], np.float32)
    gb = np.concatenate([bi[dB], bf[dB], bi[dA], bf[dA]]).reshape(1, 16)
    inv = (ROPE_THETA ** (-np.arange(0, ADH, 2, dtype=np.float32) / ADH)).astype(np.float32)
    ang = (pos[:, None] * inv[None, :]).astype(np.float32)
    cos = np.cos(ang).astype(np.float32)
    sin = np.sin(ang).astype(np.float32)
    pi = np.arange(128)
    ropec = np.ascontiguousarray(cos[:, pi % 32].T)
    sgn = np.where((pi % 64) < 32, -1.0, 1.0).astype(np.float32)
    ropes = np.ascontiguousarray((sin[:, pi % 32] * sgn[None, :]).T)
    m = {k: v for k, v in sh.items() if not k.startswith("_")}
    m["x"] = np.ascontiguousarray(x)
    m["win_tm"] = np.stack([_tiles_mov(wtm[:, n * 512:(n + 1) * 512], KC) for n in range(6)])
    m["win_tg"] = _tiles_mov(wtm[:, 3072:3088], KC)
    m["gbias"] = gb
    m["ropec"] = ropec
    m["ropes"] = ropes
    return split_map(m)


def split_map(m):
    out = {}
    for k, v in m.items():
        if k in SPLIT_PER:
            per = SPLIT_PER[k]
            for i in range(v.shape[0] // per):
                out["%s_%d" % (k, i)] = np.ascontiguousarray(v[i * per:(i + 1) * per])
        else:
            out[k] = v
    return out


N_CORES = 8


def kernel(**inputs):
    x = np.asarray(inputs["x"])
    B, S, _ = x.shape
    sh = prep_shared(inputs)
    if N_CORES == 8:
        T2, T = S, S // 2
        cores = [(b, h) for b in range(B) for h in range(2)]
    else:
        T2, T = S, S
        cores = [(b, 0) for b in range(B)]
    nc, P = build(T2, T)
    maps = [prep_core(inputs, sh, b, h, S) for (b, h) in cores]
    maps = [{k: v for k, v in m.items() if k in P.used_inputs} for m in maps]
    res = run_bass_kernel_spmd(nc, maps, core_ids=list(range(len(cores))))
    out = np.empty((B, S, D), np.float32)
    for i, (b, h) in enumerate(cores):
        o = np.asarray(res.results[i]["out"], np.float32)
        if N_CORES == 8:
            if h == 0:
                out[b, :T] = o
            else:
                out[b, T:] = o[::-1]
        else:
            out[b] = o
    return out
```
